# Optimizing a Trainium2 kernel written in Bass

```python
import jax, jax.numpy as jnp
from jax import lax
import numpy as np

D_MODEL = 1024
BATCH = 16
SEQ = 4096
DEPTH = 1

N_META = 16
CHUNK = 64
N_PAD = CHUNK - N_META
DN_HEADS = 4
DN_DK = 128
DN_DV = 128
CONV_K = 4
GLA_HEADS = 4
GLA_DK = 64
GLA_DV = 128
GLA_RANK = 16
GLA_NORMALIZER = 16.0
D_FF = 4 * D_MODEL
EPS = 1e-6

DN_QK = DN_HEADS * DN_DK
DN_V = DN_HEADS * DN_DV
GLA_QK = GLA_HEADS * GLA_DK
GLA_V = GLA_HEADS * GLA_DV
MIX_WIDTH = DN_V + GLA_V
SPLITS = (DN_QK, DN_QK, DN_V, DN_V, DN_HEADS, DN_HEADS, GLA_QK, GLA_QK, GLA_V, GLA_V, GLA_RANK)
IN_WIDTH = DN_QK * 2 + DN_V * 2 + DN_HEADS * 2 + GLA_QK * 2 + GLA_V * 2 + GLA_RANK

kernel_name = "hymba_gdn_gla_hybrid"


def rmsnorm(x, g):
    xf = x.astype(jnp.float32)
    y = xf * lax.rsqrt(jnp.mean(xf * xf, axis=-1, keepdims=True) + EPS)
    return (y * g.astype(jnp.float32)).astype(x.dtype)


def l2norm(x):
    xf = x.astype(jnp.float32)
    return xf * lax.rsqrt(jnp.sum(xf * xf, axis=-1, keepdims=True) + EPS)


def causal_conv(x, w):
    K = w.shape[0]
    T = x.shape[1]
    xp = jnp.pad(x, ((0, 0), (K - 1, 0), (0, 0)))
    y = xp[:, 0:T] * w[0]
    for i in range(1, K):
        y = y + xp[:, i:i + T] * w[i]
    return y


def to_head_chunks(x, n_heads):
    B, T, W = x.shape
    return x.reshape(B, T // CHUNK, CHUNK, n_heads, W // n_heads).transpose(0, 3, 1, 2, 4)


def scalar_chunks(x):
    B, T, H = x.shape
    return x.reshape(B, T // CHUNK, CHUNK, H).transpose(0, 3, 1, 2)


def from_head_chunks(o):
    B, H, N, C, d = o.shape
    return o.transpose(0, 2, 3, 1, 4).reshape(B, N * C, H, d)


def gated_delta_chunked(q, k, v, beta, g):
    B, H, N, C, dk = q.shape
    dv = v.shape[-1]
    q = q * (dk ** -0.5)
    gc = jnp.cumsum(g, axis=-1)
    tril = jnp.tril(jnp.ones((C, C), dtype=bool))
    strict = jnp.tril(jnp.ones((C, C), dtype=bool), -1)
    decay = jnp.exp(jnp.where(tril, gc[..., :, None] - gc[..., None, :], -jnp.inf))
    kb = k * beta[..., None]
    a = jnp.einsum('bhncd,bhnsd->bhncs', kb, k) * decay
    m = jnp.eye(C, dtype=jnp.float32) + jnp.where(strict, a, 0.0)
    rhs = jnp.concatenate([v * beta[..., None], kb * jnp.exp(gc)[..., None]], axis=-1)
    sol = lax.linalg.triangular_solve(m, rhs, left_side=True, lower=True, unit_diagonal=True)
    u = sol[..., :dv]
    w = sol[..., dv:]
    attn = jnp.einsum('bhncd,bhnsd->bhncs', q, k) * decay
    qg = q * jnp.exp(gc)[..., None]
    kd = k * jnp.exp(gc[..., -1:] - gc)[..., None]
    glast = jnp.exp(gc[..., -1])

    def step(S, xs):
        u_c, w_c, attn_c, qg_c, kd_c, gl_c = xs
        v_new = u_c - jnp.einsum('bhcd,bhde->bhce', w_c, S)
        o = jnp.einsum('bhcd,bhde->bhce', qg_c, S) + jnp.einsum('bhcs,bhse->bhce', attn_c, v_new)
        S = S * gl_c[..., None, None] + jnp.einsum('bhcd,bhce->bhde', kd_c, v_new)
        return S, o

    xs = (jnp.moveaxis(u, 2, 0), jnp.moveaxis(w, 2, 0), jnp.moveaxis(attn, 2, 0),
          jnp.moveaxis(qg, 2, 0), jnp.moveaxis(kd, 2, 0), jnp.moveaxis(glast, 2, 0))
    S0 = jnp.zeros((B, H, dk, dv), jnp.float32)
    _, o = lax.scan(step, S0, xs)
    return jnp.moveaxis(o, 0, 2)


def gla_chunked(q, k, v, g):
    B, H, N, C, dk = q.shape
    dv = v.shape[-1]
    q = q * (dk ** -0.5)
    b = jnp.cumsum(g, axis=-2)
    bref = b[..., C // 2:C // 2 + 1, :]
    qi = q * jnp.exp(b - bref)
    ki = k * jnp.exp(bref - b)
    tril = jnp.tril(jnp.ones((C, C), dtype=bool))
    A = jnp.where(tril, jnp.einsum('bhncd,bhnsd->bhncs', qi, ki), 0.0)
    o_intra = jnp.einsum('bhncs,bhnse->bhnce', A, v)
    qg = q * jnp.exp(b)
    kd = k * jnp.exp(b[..., -1:, :] - b)
    glast = jnp.exp(b[..., -1, :])

    def step(S, xs):
        qg_c, kd_c, v_c, gl_c = xs
        o = jnp.einsum('bhcd,bhde->bhce', qg_c, S)
        S = S * gl_c[..., None] + jnp.einsum('bhcd,bhce->bhde', kd_c, v_c)
        return S, o

    xs = (jnp.moveaxis(qg, 2, 0), jnp.moveaxis(kd, 2, 0), jnp.moveaxis(v, 2, 0), jnp.moveaxis(glast, 2, 0))
    S0 = jnp.zeros((B, H, dk, dv), jnp.float32)
    _, o_inter = lax.scan(step, S0, xs)
    return o_intra + jnp.moveaxis(o_inter, 0, 2)


def hybrid_layer(x, valid, norm1_g, w_in, conv_w, a_log, dt_bias, dn_norm_g,
                 gla_w2, gla_b, gla_norm_g, w_out, norm2_g, w_up, w_down):
    B, T, _ = x.shape
    vmask = valid[None, :, None]
    h = jnp.where(vmask, rmsnorm(x, norm1_g), 0).astype(x.dtype)
    proj = h @ w_in
    offs = [0]
    for s in SPLITS:
        offs.append(offs[-1] + s)
    (dq, dk_, dv, dz, db, da, gq, gk, gv, gr, glr) = [proj[..., offs[i]:offs[i + 1]] for i in range(len(SPLITS))]

    qkv = jax.nn.silu(causal_conv(jnp.concatenate([dq, dk_, dv], axis=-1), conv_w))
    q_dn = l2norm(to_head_chunks(qkv[..., :DN_QK], DN_HEADS))
    k_dn = l2norm(to_head_chunks(qkv[..., DN_QK:2 * DN_QK], DN_HEADS))
    v_dn = to_head_chunks(qkv[..., 2 * DN_QK:], DN_HEADS).astype(jnp.float32)
    beta = jax.nn.sigmoid(db.astype(jnp.float32))
    g_dn = -jnp.exp(a_log.astype(jnp.float32)) * jax.nn.softplus(da.astype(jnp.float32) + dt_bias.astype(jnp.float32))
    g_dn = jnp.where(vmask, g_dn, 0.0)
    o_dn = from_head_chunks(gated_delta_chunked(q_dn, k_dn, v_dn, scalar_chunks(beta), scalar_chunks(g_dn)))
    o_dn = rmsnorm(o_dn, dn_norm_g) * jax.nn.silu(dz.reshape(B, T, DN_HEADS, DN_DV).astype(jnp.float32))
    o_dn = o_dn.reshape(B, T, DN_V)

    g_gla = jax.nn.log_sigmoid((glr @ gla_w2 + gla_b).astype(jnp.float32)) / GLA_NORMALIZER
    g_gla = jnp.where(vmask, g_gla, 0.0)
    o_gla = gla_chunked(to_head_chunks(gq, GLA_HEADS).astype(jnp.float32),
                        to_head_chunks(gk, GLA_HEADS).astype(jnp.float32),
                        to_head_chunks(gv, GLA_HEADS).astype(jnp.float32),
                        to_head_chunks(g_gla, GLA_HEADS))
    o_gla = rmsnorm(from_head_chunks(o_gla), gla_norm_g) * jax.nn.silu(gr.reshape(B, T, GLA_HEADS, GLA_DV).astype(jnp.float32))
    o_gla = o_gla.reshape(B, T, GLA_V)

    mix = jnp.concatenate([o_dn, o_gla], axis=-1).astype(x.dtype)
    x = x + mix @ w_out

    h2 = rmsnorm(x, norm2_g)
    x = x + jnp.square(jax.nn.relu(h2 @ w_up)) @ w_down
    return x


def setup_inputs(seed: int = 0) -> dict:
    key = jax.random.key(seed)
    ks = jax.random.split(key, 20)
    f32 = jnp.float32
    x = jax.random.normal(ks[0], (BATCH, SEQ, D_MODEL), f32)
    meta_tokens = jax.random.normal(ks[1], (N_META, D_MODEL), f32)
    norm1_g = 1.0 + 0.02 * jax.random.normal(ks[2], (DEPTH, D_MODEL), f32)
    w_in = jax.random.normal(ks[3], (DEPTH, D_MODEL, IN_WIDTH), f32) * D_MODEL ** -0.5
    conv_w = jax.random.normal(ks[4], (DEPTH, CONV_K, 2 * DN_QK + DN_V), f32) * CONV_K ** -0.5
    a_log = jnp.log(jax.random.uniform(ks[5], (DEPTH, DN_HEADS), f32, 1.0, 16.0))
    dt = jnp.exp(jax.random.uniform(ks[6], (DEPTH, DN_HEADS), f32, math_log(0.001), math_log(0.1)))
    dt_bias = dt + jnp.log(-jnp.expm1(-dt))
    dn_norm_g = 1.0 + 0.02 * jax.random.normal(ks[7], (DEPTH, DN_DV), f32)
    gla_w2 = jax.random.normal(ks[8], (DEPTH, GLA_RANK, GLA_QK), f32) * GLA_RANK ** -0.5
    gla_b = 0.01 * jax.random.normal(ks[9], (DEPTH, GLA_QK), f32)
    gla_norm_g = 1.0 + 0.02 * jax.random.normal(ks[10], (DEPTH, GLA_DV), f32)
    w_out = jax.random.normal(ks[11], (DEPTH, MIX_WIDTH, D_MODEL), f32) * MIX_WIDTH ** -0.5
    norm2_g = 1.0 + 0.02 * jax.random.normal(ks[12], (DEPTH, D_MODEL), f32)
    w_up = jax.random.normal(ks[13], (DEPTH, D_MODEL, D_FF), f32) * D_MODEL ** -0.5
    w_down = jax.random.normal(ks[14], (DEPTH, D_FF, D_MODEL), f32) * D_FF ** -0.5
    final_norm_g = 1.0 + 0.02 * jax.random.normal(ks[15], (D_MODEL,), f32)
    return {"x": x, "meta_tokens": meta_tokens, "norm1_g": norm1_g, "w_in": w_in, "conv_w": conv_w,
            "a_log": a_log, "dt_bias": dt_bias, "dn_norm_g": dn_norm_g, "gla_w2": gla_w2, "gla_b": gla_b,
            "gla_norm_g": gla_norm_g, "w_out": w_out, "norm2_g": norm2_g, "w_up": w_up, "w_down": w_down,
            "final_norm_g": final_norm_g}


def math_log(v):
    return float(np.log(v))


def reference(x, meta_tokens, norm1_g, w_in, conv_w, a_log, dt_bias, dn_norm_g, gla_w2, gla_b,
              gla_norm_g, w_out, norm2_g, w_up, w_down, final_norm_g):
    B = x.shape[0]
    pad = jnp.zeros((B, N_PAD, D_MODEL), x.dtype)
    meta = jnp.broadcast_to(meta_tokens.astype(x.dtype)[None], (B, N_META, D_MODEL))
    h = jnp.concatenate([pad, meta, x], axis=1)
    T = h.shape[1]
    valid = jnp.arange(T) >= N_PAD
    for l in range(DEPTH):
        h = hybrid_layer(h, valid, norm1_g[l], w_in[l], conv_w[l], a_log[l], dt_bias[l], dn_norm_g[l],
                         gla_w2[l], gla_b[l], gla_norm_g[l], w_out[l], norm2_g[l], w_up[l], w_down[l])
    return rmsnorm(h, final_norm_g)[:, CHUNK:]
```

```python
import math
from contextlib import ExitStack

import numpy as np
import concourse.bass as bass
import concourse.mybir as mybir
from concourse.bass_utils import run_bass_kernel_spmd

F32 = mybir.dt.float32
BF16 = mybir.dt.bfloat16
AF = mybir.ActivationFunctionType
ALU = mybir.AluOpType
AX = mybir.AxisListType

D = 1024
SEQ = 4096
NCORES = 8
IN_W = 3608
DFF = 4096
EPS = 1e-6
O_DQ, O_DK, O_DV, O_DZ, O_DB, O_GQ, O_GK, O_GV, O_GR, O_GLR = 0, 512, 1024, 1536, 2048, 2056, 2312, 2568, 3080, 3592
NGRP = 16
C_ID, C_MUI, C_MSU, C_NEG, C_SEL, C_RM, C_END = 0, 128, 640, 1152, 1664, 2176, 2180


class Trk:
    __slots__ = ("w", "rs", "psum")

    def __init__(self, psum=False):
        self.w = []
        self.rs = []
        self.psum = psum


SLACK = 0.6


class Sched:
    ENG = ("pe", "act", "dve", "pool", "sp")
    LIST_SCHED = True

    def __init__(self):
        self.ops = []
        self.segs = [0]
        self.rec = None
        self.dmakeys = []
        self.nins = 0

    def _eng(self, i):
        return self.ops[i][0]

    def _record(self, e, key, emit, dur, reads, writes, acc):
        i = len(self.ops)
        deps, order = set(), set()
        for t in reads:
            deps.update(t.w)
            if t.psum:
                for r in t.rs:
                    if self._eng(r) != e and self._eng(r) != "pe":
                        deps.add(r)
        for t in writes:
            for m in t.w:
                if acc and self._eng(m) == e and self.ops[m][1] is None:
                    order.add(m)
                else:
                    deps.add(m)
            deps.update(t.rs)
        self.ops.append([e, key, emit, dur, deps, order - deps])
        self.nins += 1
        for t in reads:
            t.rs.append(i)
        for t in writes:
            if acc:
                t.w = [m for m in t.w if self._eng(m) != e] + [i]
            else:
                t.w = [i]
            t.rs = []

    def op(self, e, emit, reads=(), writes=(), acc=False, signal=True, dur=0.3):
        self._record(e, None, emit, dur, reads, writes, acc)

    def dma(self, e, key, emit, reads=(), writes=(), dur=3.0):
        if key not in self.dmakeys:
            self.dmakeys.append(key)
        self._record(e, key, emit, dur, reads, writes, False)

    def barrier(self):
        self.segs.append(len(self.ops))

    def _schedule_segment(self, lo, hi, t0):
        import heapq
        ops = self.ops
        n = hi - lo
        succ = [[] for _ in range(n)]
        indeg = [0] * n
        for i in range(lo, hi):
            for p in ops[i][4] | ops[i][5]:
                if p >= lo:
                    succ[p - lo].append(i)
                    indeg[i - lo] += 1
        prio = [0.0] * n
        for i in range(hi - 1, lo - 1, -1):
            m = 0.0
            for sidx in succ[i - lo]:
                if prio[sidx - lo] > m:
                    m = prio[sidx - lo]
            prio[i - lo] = m + ops[i][3]
        ready_t = [t0] * n
        finish = {}
        efree = {e: t0 for e in self.ENG}
        wait_h = {e: [] for e in self.ENG}
        prio_h = {e: [] for e in self.ENG}
        order = {e: [] for e in self.ENG}
        last_on = {e: None for e in self.ENG}
        for i in range(lo, hi):
            if indeg[i - lo] == 0:
                heapq.heappush(wait_h[ops[i][0]], (ready_t[i - lo], i))
        done = 0
        tmax = t0
        while done < n:
            best = None
            for e in self.ENG:
                wh, ph = wait_h[e], prio_h[e]
                while wh and wh[0][0] <= efree[e]:
                    _, i = heapq.heappop(wh)
                    heapq.heappush(ph, (-prio[i - lo], i))
                if ph and wh and wh[0][0] - efree[e] < SLACK and -prio[wh[0][1] - lo] < ph[0][0] - 1.0:
                    cand = (wh[0][0], 1, -prio[wh[0][1] - lo], e)
                elif ph:
                    cand = (efree[e], 0, ph[0][0], e)
                elif wh:
                    cand = (wh[0][0], 1, 0.0, e)
                else:
                    continue
                if best is None or cand < best:
                    best = cand
            start, kind, _, e = best
            if kind == 0:
                _, i = heapq.heappop(prio_h[e])
            else:
                _, i = heapq.heappop(wait_h[e])
            dur = ops[i][3]
            if ops[i][1] is not None:
                efree[e] = start + 0.07
            else:
                efree[e] = start + dur
            fin = start + dur
            finish[i] = fin
            if self.rec is not None:
                self.rec[i] = (start, fin, ready_t[i - lo], last_on[e])
            last_on[e] = i
            tmax = max(tmax, fin)
            order[e].append(i)
            done += 1
            for sidx in succ[i - lo]:
                k = sidx - lo
                lat = 0.04 if ops[sidx][0] == e else 0.13
                if fin + lat > ready_t[k]:
                    ready_t[k] = fin + lat
                indeg[k] -= 1
                if indeg[k] == 0:
                    heapq.heappush(wait_h[ops[sidx][0]], (ready_t[k], sidx))
        return order, tmax

    def finalize(self):
        ops = self.ops
        bounds = self.segs + [len(ops)]
        eng_order = {e: [] for e in self.ENG}
        seg_end = {e: [] for e in self.ENG}
        t0 = 0.0
        for si in range(len(bounds) - 1):
            lo, hi = bounds[si], bounds[si + 1]
            if self.LIST_SCHED:
                order, t0 = self._schedule_segment(lo, hi, t0)
            else:
                order = {e: [i for i in range(lo, hi) if ops[i][0] == e] for e in self.ENG}
            for e in self.ENG:
                eng_order[e].extend(order[e])
                seg_end[e].append(len(eng_order[e]))
        pos = {}
        for e in self.ENG:
            for k_, i in enumerate(eng_order[e]):
                pos[i] = k_
        semof = lambda i: ops[i][1] if ops[i][1] is not None else ops[i][0]
        needed = [False] * len(ops)
        best_preds = [None] * len(ops)
        for i, o in enumerate(ops):
            best = {}
            for p in o[4]:
                if o[0] == "pe" and ops[p][0] == "pe" and ops[p][1] is None and o[1] is None:
                    continue
                k = semof(p)
                if k not in best or pos[p] > pos[best[k]]:
                    best[k] = p
            best_preds[i] = list(best.values())
            for p in best_preds[i]:
                needed[p] = True
        last_in_seg = []
        for si in range(len(bounds) - 2):
            last = {}
            for e in self.ENG:
                lo_pos = seg_end[e][si - 1] if si > 0 else 0
                for i in eng_order[e][lo_pos:seg_end[e][si]]:
                    last[semof(i)] = i
            for i in last.values():
                needed[i] = True
            last_in_seg.append(last)
        cnt = {}
        count_of = {}
        for e in self.ENG:
            for i in eng_order[e]:
                if needed[i]:
                    k = semof(i)
                    cnt[k] = cnt.get(k, 0) + 1
                    count_of[i] = cnt[k]
        prog = {}
        for e in self.ENG:
            seen = {}
            out = []
            seg_i = 0
            for pos, i in enumerate(eng_order[e]):
                while seg_i < len(last_in_seg) and pos >= seg_end[e][seg_i]:
                    bw = []
                    for k, j in last_in_seg[seg_i].items():
                        if k != e and seen.get(k, 0) < count_of[j]:
                            seen[k] = count_of[j]
                            bw.append((k, count_of[j]))
                    if bw:
                        out.append((bw, None, None))
                    seg_i += 1
                need = {}
                for p in best_preds[i]:
                    k = semof(p)
                    if need.get(k, 0) < count_of[p]:
                        need[k] = count_of[p]
                waits = []
                for k, c in need.items():
                    if seen.get(k, 0) < c:
                        seen[k] = c
                        waits.append((k, c))
                out.append((waits, ops[i][2], semof(i) if needed[i] else None))
            prog[e] = out
        self.prog = prog
        return prog

    def replay(self, e, eng, sems):
        for waits, emit, sig in self.prog[e]:
            for en, c in waits:
                eng.wait_ge(sems[en], c * (1 if en in self.ENG else 16))
            if emit is None:
                continue
            ins = emit(eng)
            if sig is not None:
                ins.then_inc(sems[sig], 1 if sig in self.ENG else 16)


import os
DBG_STOP = int(os.environ.get('DBG_STOP', '1000000'))


def build_program(n_seq, macros, debug=False):
    n_real = sum(macros) * 128
    nc = bass.Bass("TRN2", target_bir_lowering=False)

    def din(name, shape):
        return nc.dram_tensor(name, list(shape), F32, kind="ExternalInput").ap()

    x_d = din("x", [n_seq, n_real, D])
    meta_d = din("meta", [16, D])
    win_d = din("w_in", [D, IN_W])
    wout_d = din("w_out", [D, D])
    wup_d = din("w_up", [D, DFF])
    wdn_d = din("w_down", [DFF, D])
    conv_d = din("conv_w", [4, 1536])
    cst_d = din("consts", [128, C_END])
    g1_d = din("norm1_g", [D])
    g2_d = din("norm2_g", [D])
    gf_d = din("final_norm_g", [D])
    gdn_d = din("dn_norm_g", [128])
    ggl_d = din("gla_norm_g", [128])
    alog_d = din("a_log", [4])
    dtb_d = din("dt_bias", [4])
    w2_d = din("gla_w2", [16, 256])
    gb_d = din("gla_b", [256])
    out_d = nc.dram_tensor("out", [n_seq, n_real, D], F32, kind="ExternalOutput").ap()
    wup_s = nc.dram_tensor("wup_s", [NGRP, 128, 8 * 256], BF16, kind="Internal").ap()
    wdn_s = nc.dram_tensor("wdn_s", [NGRP, 128, 2 * 1024], BF16, kind="Internal").ap()

    S = Sched()
    SBTOT = [0]
    es = ExitStack()
    TR = {}

    def sb(name, free, dt, parts=128):
        t = es.enter_context(nc.sbuf_tensor(name, [parts, free], dt))
        TR[name] = Trk()
        SBTOT[0] += free * (2 if dt == BF16 else 4)
        return t

    def _free(ap):
        n = 1
        for d in ap.shape[1:]:
            n *= int(d)
        return n

    def I(e, method, reads, writes, acc=False, signal=True, **kw):
        ap = kw.get("out", kw.get("ap"))
        n = _free(ap) if ap is not None else 128
        aps = [v for v in kw.values() if hasattr(v, "dtype") and hasattr(v, "shape")]
        all16 = all(v.dtype == BF16 for v in aps)
        if e == "act":
            dur = 0.22 + n / 1100.0
        elif e == "dve":
            rate = 1900.0 if all16 else 900.0
            dur = 0.12 + n / rate * (2.0 if method == "tensor_tensor_scan" else 1.0)
        else:
            dur = 0.3 + n / 150.0
        S.op(e, lambda eng, m=method, kw=kw: getattr(eng, m)(**kw), reads=reads, writes=writes, acc=acc, signal=signal, dur=dur)

    def MM(out, lhsT, rhs, reads, writes, start=True, stop=True, acc=False, signal=True):
        n = _free(out)
        dur = max(n, 90) / 2400.0 * (4.0 if lhsT.dtype == F32 else 1.0)
        S.op("pe", lambda eng, o=out, l=lhsT, r=rhs, st=start, sp=stop: eng.matmul(o, lhsT=l, rhs=r, start=st, stop=sp),
             reads=reads, writes=writes, acc=acc, signal=signal, dur=dur)

    def TP(out, in_, ident, reads, writes, acc=False, signal=True):
        S.op("pe", lambda eng, o=out, i=in_, d=ident: eng.transpose(o, i, d), reads=reads, writes=writes, acc=acc, signal=signal, dur=0.09)

    dkeys = {}

    def DMA(key, out, in_, reads, writes, e="sp", **kw):
        key = dkeys.setdefault(id(writes[0]), "k%d" % len(dkeys))
        ap_ = out if len(out.shape) > 1 else in_
        nb = 128 * _free(ap_) * (2 if ap_.dtype == BF16 else 4)
        S.dma(e, key, lambda eng, o=out, i=in_, kw=kw: eng.dma_start(out=o, in_=i, **kw), reads=reads, writes=writes,
              dur=2.0 + nb / 150e3)

    win = sb("win", 8 * IN_W, BF16)
    wout = sb("wout", 8 * D, BF16)
    Dg = sb("Dg", 48 * 128, BF16)
    identf = sb("identf", 128, F32)
    identb = sb("identb", 128, BF16)
    onesb = sb("onesb", 128, BF16)
    onesf = sb("onesf", 128, F32)
    I4b = sb("I4b", 512, BF16)
    mUI = sb("mUI", 512, BF16)
    mSU = sb("mSU", 512, BF16)
    gfbc = sb("gfbc", 1024, F32)
    w2a = sb("w2a", 256, F32)
    cols = sb("cols", 96, F32)
    XT = [sb("xt0", 1024, F32), sb("xt1", 1024, F32)]
    hb = sb("hb", 1024, BF16)
    hb2 = sb("hb2", 1024, BF16)
    hT = sb("hT", 1024, BF16)
    SA = sb("SA", 1536, F32)
    SB = sb("SB", 1024, F32)
    cvb = sb("cvb", 12 * 131, BF16)
    qkn = sb("qkn", 1024, BF16)
    sqb = sb("sqb", 1024, BF16)
    kvtm = sb("kvtm", 1024, BF16)
    vTb = sb("vTb", 512, BF16)
    g8 = sb("g8", 4 * 128, F32)
    gcol = sb("gcol", 16, F32)
    kgT = sb("kgT", 512, BF16)
    qgT = sb("qgT", 512, BF16)
    kd = sb("kd", 512, BF16)
    Xp = sb("Xp", 512, BF16)
    Pm = [sb("Pm%d" % i, 512, BF16) for i in range(2)]
    Qm = [sb("Qm%d" % i, 512, BF16) for i in range(2)]
    Tm = [sb("Tm%d" % i, 512, BF16) for i in range(2)]
    attnT = sb("attnT", 512, BF16)
    rr = sb("rr", 512, BF16)
    vnew = sb("vnew", 512, BF16)
    Sd = sb("Sd", 512, F32)
    Sdb = sb("Sdb", 512, BF16)
    glra = sb("glra", 128, F32)
    Lg = sb("Lg", 256, F32)
    Bc = sb("Bc", 256, F32)
    Eqi = sb("Eqi", 256, F32)
    Eki = sb("Eki", 256, F32)
    Ekd = sb("Ekd", 256, F32)
    Eqg = sb("Eqg", 256, F32)
    gqiT = sb("gqiT", 256, BF16)
    gkiT = [sb("gkiT%d" % r, 256, BF16) for r in range(2)]
    gqgT = [sb("gqgT%d" % r, 256, BF16) for r in range(2)]
    gkdT = sb("gkdT", 256, BF16)
    kdgtm = sb("kdgtm", 256, BF16)
    gvtm = sb("gvtm", 512, BF16)
    ATm = sb("ATm", 512, BF16)
    Sg = sb("Sg", 256, F32)
    Sgb = sb("Sgb", 256, BF16)
    mix = sb("mix", 1024, BF16)
    mixT = sb("mixT", 1024, BF16)
    U_X1, U_H2T, U_WU, U_WD, U_AT, U_RL = 0, 6144, 9216, 13312, 17408, 18944
    U_END = 19712
    U = sb("U", max(U_END, 16384), BF16)
    X1 = U[:, U_X1:U_X1 + 6144].bitcast(F32)
    H2T = U[:, U_H2T:U_H2T + 3072]
    WU = [U[:, U_WU + i * 2048:U_WU + (i + 1) * 2048] for i in range(2)]
    WD = [U[:, U_WD + i * 2048:U_WD + (i + 1) * 2048] for i in range(2)]
    ATb = [U[:, U_AT + i * 768:U_AT + (i + 1) * 768] for i in range(2)]
    RLs = [U[:, U_RL + i * 384:U_RL + (i + 1) * 384] for i in range(2)]
    stg_f = U[:, 0:8192].bitcast(F32)
    stg_b = [U[:, 8192 + i * 4096:8192 + (i + 1) * 4096] for i in range(2)]
    for nm in ["X1_0", "X1_1", "X1_2", "H2T", "WU0", "WU1", "WD0", "WD1", "AT0", "AT1", "RL0", "RL1", "stgf", "stgb0", "stgb1",
               "wup_s", "wdn_s", "SA0", "SA1", "SA2", "SB0", "SB1"]:
        TR[nm] = Trk()
    PS = es.enter_context(nc.psum_tensor("PS", [128, 4096], F32))
    PSb = PS.bitcast(BF16)
    BK = [Trk(psum=True) for _ in range(8)]

    def bank(b, lo=0, hi=512):
        return PS[:, b * 512 + lo:b * 512 + hi]

    def bankb(b, lo=0, hi=1024):
        return PSb[:, b * 1024 + lo:b * 1024 + hi]

    def col(i, n=1):
        return cols[:, i:i + n]
    CT = {k: Trk() for k in ["ss1", "rs1", "ss2", "rs2", "ss3", "rs3", "g1c", "g2c", "bcol", "dl4", "dcol", "ss4", "rs4",
                             "ss5", "rs5", "cA", "cB", "cC", "gl", "p8", "fin", "pm", "rm", "gnc", "gnc2"]}
    c_ss1, c_rs1, c_ss2, c_rs2, c_ss3, c_rs3 = col(0), col(1), col(2), col(3), col(4), col(5)
    c_g1, c_g2 = col(8, 8), col(16, 8)
    c_bcol, c_dl4, c_dcol, c_ss4, c_rs4, c_ss5, c_rs5 = col(24, 4), col(28, 4), col(32, 4), col(36, 4), col(40, 4), col(44, 4), col(48, 4)
    c_cA, c_cB, c_cC, c_gl = col(52, 2), col(54, 2), col(56, 2), col(58, 2)
    c_scl, c_bia, c_coef = col(60), col(61), col(62)
    c_fin = col(64)

    T = TR
    DMA("c0", stg_f[:, 0:C_END], cst_d, [], [T["stgf"]])
    I("dve", "tensor_copy", [T["stgf"]], [T["identf"]], out=identf[:], in_=stg_f[:, C_ID:C_ID + 128])
    I("dve", "tensor_copy", [T["stgf"]], [T["identb"]], out=identb[:], in_=stg_f[:, C_ID:C_ID + 128])
    I("dve", "tensor_copy", [T["stgf"]], [T["mUI"]], out=mUI[:], in_=stg_f[:, C_MUI:C_MUI + 512])
    I("dve", "tensor_copy", [T["stgf"]], [T["mSU"]], out=mSU[:], in_=stg_f[:, C_MSU:C_MSU + 512])
    I("dve", "tensor_copy", [T["stgf"]], [CT["rm"]], out=cols[0:8, 70:74], in_=stg_f[0:8, C_RM:C_RM + 4])
    for h in range(4):
        I("dve", "tensor_copy", [T["stgf"]], [T["I4b"]], out=I4b[:, h * 128:(h + 1) * 128], in_=stg_f[:, C_ID:C_ID + 128])
    I("pool", "memset", [], [T["onesb"]], ap=onesb[:], constant=1.0)
    I("pool", "memset", [], [T["onesf"]], ap=onesf[:], constant=1.0)
    I("pool", "memset", [], [T["glra"]], ap=glra[0:32, :], constant=1.0)
    I("pool", "memset", [], [CT["p8"]], ap=cols[:, 60:63], constant=0.0)
    I("pool", "memset", [], [CT["pm"]], ap=cols[:, 66:70], constant=0.0)
    I("pool", "memset", [CT["pm"]], [CT["pm"]], ap=cols[0:64, 66:67], constant=1.0)
    I("pool", "memset", [CT["pm"]], [CT["pm"]], ap=cols[64:128, 67:68], constant=1.0)
    I("pool", "memset", [CT["pm"]], [CT["pm"]], ap=cols[0:64, 68:69], constant=0.125)
    I("pool", "memset", [CT["pm"]], [CT["pm"]], ap=cols[64:128, 69:70], constant=0.125)
    DMA("c1", cols[:, 74:75], gdn_d.rearrange("(p o) -> p o", o=1), [], [CT["gnc"]], allow_slow_non_contiguous=True)
    DMA("c1", cols[:, 75:76], ggl_d.rearrange("(p o) -> p o", o=1), [], [CT["gnc2"]], allow_slow_non_contiguous=True)
    DMA("c1", gfbc[:], gf_d.partition_broadcast(128), [], [T["gfbc"]])
    DMA("c1", c_g1, g1_d.rearrange("(k p) -> p k", p=128), [], [CT["g1c"]], allow_slow_non_contiguous=True)
    DMA("c1", c_g2, g2_d.rearrange("(k p) -> p k", p=128), [], [CT["g2c"]], allow_slow_non_contiguous=True)
    DMA("c1", w2a[0:16, :], w2_d, [], [T["w2a"]])
    DMA("c1", w2a[16:17, :], gb_d.rearrange("(o n) -> o n", o=1), [], [T["w2a"]])
    DMA("c1", cols[4:8, 62:63], alog_d.rearrange("(p o) -> p o", o=1), [CT["p8"]], [CT["p8"]], allow_slow_non_contiguous=True)
    DMA("c1", cols[4:8, 61:62], dtb_d.rearrange("(p o) -> p o", o=1), [CT["p8"]], [CT["p8"]], allow_slow_non_contiguous=True)
    I("act", "activation", [CT["p8"]], [CT["p8"]], out=cols[0:8, 62:63], in_=cols[0:8, 62:63], func=AF.Exp)
    I("dve", "tensor_scalar", [CT["p8"]], [CT["p8"]], out=cols[0:8, 62:63], in0=cols[0:8, 62:63], scalar1=-1.0, scalar2=None, op0=ALU.mult)
    I("pool", "memset", [CT["p8"]], [CT["p8"]], ap=cols[0:8, 60:61], constant=1.0)
    I("pool", "memset", [CT["p8"]], [CT["p8"]], ap=cols[0:4, 60:61], constant=-1.0)
    DMA("c2", SB[:, 0:48].rearrange("p (i j) -> p i j", i=4), conv_d.rearrange("i (j p) -> p i j", p=128), [], [T["SB0"]],
        allow_slow_non_contiguous=True)
    for ij in range(48):
        I("dve", "tensor_scalar", [T["SB0"], T["identb"]], [T["Dg"]], acc=True,
          out=Dg[:, ij * 128:(ij + 1) * 128], in0=identb[:], scalar1=SB[:, ij:ij + 1], scalar2=None, op0=ALU.mult)
    win_v = win_d.rearrange("(k p) n -> k p n", p=128)
    for kc in range(8):
        DMA("c3", stg_f[:, 0:IN_W], win_v[kc], [], [T["stgf"]])
        I("dve", "tensor_scalar", [T["stgf"], CT["g1c"]], [T["win"]], acc=True,
          out=win[:, kc * IN_W:(kc + 1) * IN_W], in0=stg_f[:, 0:IN_W], scalar1=cols[:, 8 + kc:9 + kc], scalar2=None, op0=ALU.mult)
    wout_v = wout_d.rearrange("(k p) n -> k p n", p=128)
    for kc in range(8):
        DMA("c3", stg_f[:, 0:D], wout_v[kc], [], [T["stgf"]])
        I("dve", "tensor_scalar", [T["stgf"], CT["gnc"], CT["gnc2"]], [T["wout"]], acc=True,
          out=wout[:, kc * D:(kc + 1) * D], in0=stg_f[:, 0:D], scalar1=cols[:, 74 + kc // 4:75 + kc // 4], scalar2=None, op0=ALU.mult)
    wup_v = wup_d.rearrange("(k p) n -> k p n", p=128)
    for kc in range(8):
        sl = kc % 2
        DMA("c3", stg_f[:, 0:DFF], wup_v[kc], [], [T["stgf"]])
        I("dve", "tensor_scalar", [T["stgf"], CT["g2c"]], [T["stgb%d" % sl]],
          out=stg_b[sl][:, 0:DFF], in0=stg_f[:, 0:DFF], scalar1=cols[:, 16 + kc:17 + kc], scalar2=None, op0=ALU.mult)
        DMA("c4", wup_s[:, :, kc * 256:(kc + 1) * 256].rearrange("g p f -> p g f"),
            stg_b[sl][:, 0:DFF].rearrange("p (g f) -> p g f", f=256), [T["stgb%d" % sl]], [T["wup_s"]])
    wdn_v = wdn_d.rearrange("(g j p) n -> g p j n", p=128, j=2)
    for gg in range(8):
        sl = gg % 2
        for q in range(2):
            DMA("c3", stg_f[:, q * 2048:(q + 1) * 2048].rearrange("p (j n) -> p j n", j=2), wdn_v[gg * 2 + q], [], [T["stgf"]])
        I("dve", "tensor_copy", [T["stgf"]], [T["stgb%d" % sl]], out=stg_b[sl][:, 0:4096], in_=stg_f[:, 0:4096])
        for q in range(2):
            DMA("c4", wdn_s[gg * 2 + q], stg_b[sl][:, q * 2048:(q + 1) * 2048], [T["stgb%d" % sl]], [T["wdn_s"]])
    n_init = len(S.ops)
    for nm in ["X1_0", "X1_1", "X1_2", "H2T", "WU0", "WU1", "WD0", "WD1", "AT0", "AT1", "RL0", "RL1"]:
        T[nm].w = list(range(n_init))

    def rsqrt_cols(ss_ap, rs_ap, tss, trs, scale, parts=128):
        I("act", "activation", [tss], [trs], out=rs_ap, in_=ss_ap, func=AF.Ln, scale=scale, bias=EPS)
        I("act", "activation", [trs], [trs], out=rs_ap, in_=rs_ap, func=AF.Exp, scale=-0.5)

    def sigmoid_chain(dst, src, n, rd, wr):
        I("act", "activation", rd, wr, out=dst, in_=src, func=AF.Exp, scale=-1.0)
        I("act", "activation", wr, wr, out=dst, in_=dst, func=AF.Ln, bias=1.0)
        I("act", "activation", wr, wr, out=dst, in_=dst, func=AF.Exp, scale=-1.0)

    def proj_tm(bk, col0, ncols):
        for kc in range(8):
            MM(bank(bk, 0, ncols), hT[:, kc * 128:(kc + 1) * 128], win[:, kc * IN_W + col0:kc * IN_W + col0 + ncols],
               [T["hT"], T["win"]], [BK[bk]], start=(kc == 0), stop=(kc == 7), acc=(kc > 0), signal=(kc == 7))

    def run_parallel(gens):
        gens = list(gens)
        rounds = 0
        while gens:
            rounds += 1
            if rounds > DBG_STOP:
                return
            k = 0
            while k < len(gens):
                try:
                    r = next(gens[k])
                    if isinstance(r, tuple) and r[0] == "spawn":
                        gens.extend(r[1])
                    elif isinstance(r, tuple) and r[0] == "drain":
                        for g in r[1]:
                            for _ in g:
                                pass
                    k += 1
                except StopIteration:
                    gens.pop(k)

    SA0, SA1, SA2 = SA[:, 0:512], SA[:, 512:1024], SA[:, 1024:1536]
    TSA = [T["SA0"], T["SA1"], T["SA2"]]
    TSB = [T["SB0"], T["SB1"]]

    def mixer_tile(seq, ti, mslot, carry):
        meta_tile = (ti == 0)
        xt = XT[ti % 2]
        txt = T["xt%d" % (ti % 2)]
        flags = {"gate": False, "z": False}
        if meta_tile:
            I("pool", "memset", [], [txt], ap=xt[:], constant=0.0)
            DMA("x", xt[112:128, :], meta_d, [], [txt])
            I("pool", "memset", [], [T["cvb"]], ap=cvb[:], constant=0.0)
            I("pool", "memset", [], [T["Sd"]], ap=Sd[:], constant=0.0)
            I("pool", "memset", [], [T["Sdb"]], ap=Sdb[:], constant=0.0)
            I("pool", "memset", [], [T["Sg"]], ap=Sg[:], constant=0.0)
            I("pool", "memset", [], [T["Sgb"]], ap=Sgb[:], constant=0.0)
        else:
            DMA("x", xt[:], x_d[seq, (ti - 1) * 128:ti * 128, :], [], [txt])
        I("act", "activation", [txt], [TSA[0], TSA[1], CT["ss1"]], out=SA[:, 0:1024], in_=xt[:], func=AF.Square, accum_out=c_ss1)
        rsqrt_cols(c_ss1, c_rs1, CT["ss1"], CT["rs1"], 1.0 / D)
        I("dve", "tensor_scalar", [txt, CT["rs1"]], [T["hb"]], out=hb[:], in0=xt[:], scalar1=c_rs1, scalar2=None, op0=ALU.mult)
        for kc in range(8):
            TP(bankb(0, kc * 128, (kc + 1) * 128), hb[:, kc * 128:(kc + 1) * 128], identb[:], [T["hb"], T["identb"]], [BK[0]],
               acc=(kc > 0), signal=(kc == 7))
        I("act", "activation", [BK[0]], [T["hT"]], out=hT[:], in_=bankb(0), func=AF.Copy)
        cv3 = cvb[:].rearrange("p (j t) -> p j t", t=131)
        decI, egc = SA0, SA2

        def gate_branch():
            for kc in range(8):
                MM(bank(5, 0, 128)[0:8, :], win[:, kc * IN_W + O_DB:kc * IN_W + O_DB + 8], hT[:, kc * 128:(kc + 1) * 128],
                   [T["hT"], T["win"]], [BK[5]], start=(kc == 0), stop=(kc == 7), acc=(kc > 0), signal=(kc == 7))
            E8, G8, GC8 = g8[0:8, 0:128], g8[0:8, 128:256], g8[0:8, 256:384]
            I("act", "activation", [BK[5], CT["p8"]], [T["g8"]], out=E8, in_=bank(5, 0, 128)[0:8, :], func=AF.Exp,
              scale=cols[0:8, 60:61], bias=cols[0:8, 61:62])
            yield
            I("act", "activation", [T["g8"]], [T["g8"]], out=E8, in_=E8, func=AF.Ln, bias=1.0)
            I("dve", "tensor_scalar", [T["g8"], CT["p8"]], [T["g8"]], out=G8, in0=E8, scalar1=cols[0:8, 62:63], scalar2=None, op0=ALU.mult)
            if meta_tile:
                I("dve", "memset", [T["g8"]], [T["g8"]], ap=g8[0:8, 128:128 + 112], constant=0.0)
            I("dve", "tensor_tensor_scan", [T["g8"], T["onesf"]], [T["g8"]], out=GC8, data0=onesf[0:8, :], data1=G8, initial=0.0,
              op0=ALU.mult, op1=ALU.add)
            yield
            TP(bank(5, 128, 136), G8, identf[0:8, 0:8], [T["g8"], T["identf"]], [BK[5]], acc=True, signal=False)
            TP(bank(5, 136, 144), GC8, identf[0:8, 0:8], [T["g8"], T["identf"]], [BK[5]], acc=True)
            for h in range(4):
                I("dve", "tensor_scalar", [T["g8"], CT["rm"]], [TSA[2]], acc=True, out=SA[0:8, 1024 + h * 128:1024 + (h + 1) * 128], in0=GC8,
                  scalar1=cols[0:8, 70 + h:71 + h], scalar2=None, op0=ALU.mult)
            MM(bank(6), onesf[0:8, 0:128], SA[0:8, 1024:1536], [TSA[2], T["onesf"]], [BK[6]])
            I("dve", "tensor_copy", [BK[5]], [T["gcol"]], out=gcol[:], in_=bank(5, 128, 144))
            I("act", "activation", [T["gcol"]], [CT["bcol"]], out=c_bcol, in_=gcol[:, 0:4], func=AF.Exp)
            yield
            for h in range(4):
                I("dve", "tensor_scalar", [BK[6], T["gcol"]], [TSA[0]], acc=True, out=SA[:, h * 128:(h + 1) * 128],
                  in0=bank(6, h * 128, (h + 1) * 128), scalar1=gcol[:, 12 + h:13 + h], scalar2=0.0, op0=ALU.subtract, op1=ALU.min)
            I("act", "activation", [BK[6], TSA[0]], [TSA[2]], out=egc, in_=bank(6), func=AF.Exp)
            yield
            I("act", "activation", [TSA[0]], [TSA[0]], out=decI, in_=decI, func=AF.Exp)
            I("dve", "tensor_tensor", [T["gcol"], BK[6], TSA[2]], [CT["dl4"]], out=c_dl4, in0=gcol[:, 12:16],
              in1=bank(6).rearrange("p (h c) -> p h c", c=128)[:, :, 127], op=ALU.subtract)
            I("act", "activation", [CT["dl4"]], [CT["dcol"]], out=c_dcol, in_=c_dl4, func=AF.Exp, scale=-1.0)
            I("dve", "tensor_tensor", [TSA[0], T["mUI"]], [TSA[0]], out=decI, in0=decI, in1=mUI[:], op=ALU.mult)
            flags["gate"] = True
            yield

        def gla_branch():
            for c in range(4):
                col0 = O_GQ + c * 128
                for kc in range(8):
                    MM(bank(1, c * 128, (c + 1) * 128), win[:, kc * IN_W + col0:kc * IN_W + col0 + 128], hT[:, kc * 128:(kc + 1) * 128],
                       [T["hT"], T["win"]], [BK[1]], start=(kc == 0), stop=(kc == 7), acc=(kc > 0 or c > 0), signal=(kc == 7 and c == 3))
                if c % 2 == 1:
                    yield
            for kc in range(8):
                MM(bank(2, 256, 384)[0:16, :], win[:, kc * IN_W + O_GLR:kc * IN_W + O_GLR + 16], hT[:, kc * 128:(kc + 1) * 128],
                   [T["hT"], T["win"]], [BK[2]], start=(kc == 0), stop=(kc == 7), acc=(kc > 0), signal=(kc == 7))
            I("act", "activation", [BK[2]], [T["glra"]], out=glra[0:16, :], in_=bank(2, 256, 384)[0:16, :], func=AF.Copy)
            yield
            for c in range(2):
                MM(bank(2, c * 128, (c + 1) * 128), w2a[0:17, c * 128:(c + 1) * 128], glra[0:17, :], [T["w2a"], T["glra"]], [BK[2]],
                   acc=True, signal=(c == 1))
            I("act", "activation", [BK[2]], [T["Lg"]], out=Lg[:], in_=bank(2, 0, 256), func=AF.Exp, scale=-1.0)
            yield
            I("act", "activation", [T["Lg"]], [T["Lg"]], out=Lg[:], in_=Lg[:], func=AF.Ln, bias=1.0)
            if meta_tile:
                I("dve", "memset", [T["Lg"]], [T["Lg"]], ap=Lg[:].rearrange("p (c t) -> p c t", c=2)[:, :, 0:112], constant=0.0)
            for c in range(2):
                I("dve", "tensor_tensor_scan", [T["Lg"], T["onesf"]], [T["Bc"]], acc=True, out=Bc[:, c * 128:(c + 1) * 128], data0=onesf[:],
                  data1=Lg[:, c * 128:(c + 1) * 128], initial=0.0, op0=ALU.mult, op1=ALU.add)
            yield
            Bc3 = Bc[:].rearrange("p (c t) -> p c t", c=2)
            I("dve", "tensor_scalar", [T["Bc"]], [CT["cA"]], out=c_cA, in0=Bc3[:, :, 64], scalar1=1.0 / 16, scalar2=None, op0=ALU.mult)
            I("dve", "tensor_scalar", [T["Bc"]], [CT["cB"]], out=c_cB, in0=Bc3[:, :, 64], scalar1=-1.0 / 16, scalar2=None, op0=ALU.mult)
            I("dve", "tensor_scalar", [T["Bc"]], [CT["cC"]], out=c_cC, in0=Bc3[:, :, 127], scalar1=-1.0 / 16, scalar2=None, op0=ALU.mult)
            yield
            for c in range(2):
                s_ = slice(c * 128, (c + 1) * 128)
                if not meta_tile:
                    I("act", "activation", [T["Bc"], CT["cA"]], [T["Eqi"]], acc=True, out=Eqi[:, s_], in_=Bc[:, s_], func=AF.Exp,
                      scale=-1.0 / 16, bias=c_cA[:, c:c + 1])
                    I("act", "activation", [T["Bc"], CT["cB"]], [T["Eki"]], acc=True, out=Eki[:, s_], in_=Bc[:, s_], func=AF.Exp,
                      scale=1.0 / 16, bias=c_cB[:, c:c + 1])
                I("act", "activation", [T["Bc"], CT["cC"]], [T["Ekd"]], acc=True, out=Ekd[:, s_], in_=Bc[:, s_], func=AF.Exp,
                  scale=1.0 / 16, bias=c_cC[:, c:c + 1])
                yield
            I("act", "activation", [CT["cC"]], [CT["gl"]], out=c_gl, in_=c_cC, func=AF.Exp)
            if not meta_tile:
                I("act", "activation", [T["Bc"]], [T["Eqg"]], out=Eqg[:], in_=Bc[:], func=AF.Exp, scale=-1.0 / 16)
                I("dve", "scalar_tensor_tensor", [BK[1], T["Eqi"]], [T["gqiT"]], out=gqiT[:], in0=bank(1, 0, 256), scalar=0.125, in1=Eqi[:],
                  op0=ALU.mult, op1=ALU.mult)
                yield
                for r in range(2):
                    I("dve", "scalar_tensor_tensor", [BK[1], T["Eki"], CT["pm"]], [T["gkiT%d" % r]], out=gkiT[r][:], in0=bank(1, 256, 512),
                      scalar=cols[:, 66 + r:67 + r], in1=Eki[:], op0=ALU.mult, op1=ALU.mult)
                yield
                for r in range(2):
                    I("dve", "scalar_tensor_tensor", [BK[1], T["Eqg"], CT["pm"]], [T["gqgT%d" % r]], out=gqgT[r][:], in0=bank(1, 0, 256),
                      scalar=cols[:, 68 + r:69 + r], in1=Eqg[:], op0=ALU.mult, op1=ALU.mult)
            I("dve", "tensor_tensor", [BK[1], T["Ekd"]], [T["gkdT"]], out=gkdT[:], in0=bank(1, 256, 512), in1=Ekd[:], op=ALU.mult)
            yield
            for c in range(2):
                TP(bankb(2, 768 + c * 128, 768 + (c + 1) * 128), gkdT[:, c * 128:(c + 1) * 128], identb[:], [T["gkdT"], T["identb"]], [BK[2]],
                   acc=True, signal=(c == 1))
            I("act", "activation", [BK[2]], [T["kdgtm"]], out=kdgtm[:], in_=bankb(2, 768, 1024), func=AF.Copy)
            yield
            proj_tm(2, O_GV, 512)
            I("act", "activation", [BK[2]], [T["gvtm"]], out=gvtm[:], in_=bank(2), func=AF.Copy)
            yield
            if not meta_tile:
                for h in range(4):
                    c, r = h // 2, h % 2
                    MM(bank(2, h * 128, (h + 1) * 128), gkiT[r][:, c * 128:(c + 1) * 128],
                       gqiT[:, c * 128:(c + 1) * 128], [T["gkiT%d" % r], T["gqiT"]], [BK[2]], acc=(h > 0), signal=(h == 3))
                I("dve", "tensor_tensor", [BK[2], T["mUI"]], [T["ATm"]], out=ATm[:], in0=bank(2), in1=mUI[:], op=ALU.mult)
                yield
                for h in range(4):
                    c, r = h // 2, h % 2
                    s_ = slice(h * 128, (h + 1) * 128)
                    MM(bank(1, h * 128, (h + 1) * 128), ATm[:, s_], gvtm[:, s_], [T["ATm"], T["gvtm"]], [BK[1]], start=True, stop=False,
                       acc=(h > 0), signal=False)
                    MM(bank(1, h * 128, (h + 1) * 128), gqgT[r][:, c * 128:(c + 1) * 128],
                       Sgb[:, c * 128:(c + 1) * 128], [T["gqgT%d" % r], T["Sgb"]], [BK[1]], start=False, stop=True,
                       acc=True, signal=(h == 3))
                yield
            for c in range(2):
                MM(bank(2, c * 256, (c + 1) * 256), kdgtm[:, c * 128:(c + 1) * 128], gvtm[:, c * 256:(c + 1) * 256],
                   [T["kdgtm"], T["gvtm"]], [BK[2]], acc=(c > 0), signal=(c == 1))
            yield
            for c in range(2):
                for r in range(2):
                    pr = slice(r * 64, (r + 1) * 64)
                    I("dve", "scalar_tensor_tensor", [BK[2], CT["gl"], T["Sg"]], [T["Sg"]], acc=True, out=Sg[pr, c * 128:(c + 1) * 128],
                      in0=Sg[pr, c * 128:(c + 1) * 128], scalar=cols[pr, 58 + c:59 + c],
                      in1=PS[pr, 2 * 512 + c * 256 + r * 128:2 * 512 + c * 256 + (r + 1) * 128], op0=ALU.mult, op1=ALU.add)
            I("pool", "tensor_copy", [T["Sg"]], [T["Sgb"]], out=Sgb[:], in_=Sg[:])
            yield

        def z_branch():
            for zi, zcol0 in enumerate((O_DZ, O_GR)):
                dst = SB[:, zi * 512:(zi + 1) * 512]
                proj_tm(3, zcol0, 512)
                yield
                sigmoid_chain(dst, bank(3), 512, [BK[3]], [TSB[zi]])
                yield
                I("dve", "tensor_tensor", [BK[3], TSB[zi]], [TSB[zi]], out=dst, in0=bank(3), in1=dst, op=ALU.mult)
                yield
            flags["z"] = True

        def norm_gate(obk, zi, mixoff, c_ss, c_rs, tss, trs):
            I("act", "activation", [BK[obk]], [TSA[1]], out=SA1, in_=bank(obk), func=AF.Square)
            I("dve", "tensor_reduce", [TSA[1]], [tss], out=c_ss, in_=SA1.rearrange("p (h e) -> p h e", h=4), axis=AX.X, op=ALU.add)
            rsqrt_cols(c_ss, c_rs, tss, trs, 1.0 / 128)
            for h in range(4):
                I("dve", "scalar_tensor_tensor", [BK[obk], trs, TSB[zi]], [T["mix"]], acc=True,
                  out=mix[:, mixoff + h * 128:mixoff + (h + 1) * 128], in0=bank(obk, h * 128, (h + 1) * 128),
                  scalar=c_rs[:, h:h + 1], in1=SB[:, zi * 512 + h * 128:zi * 512 + (h + 1) * 128], op0=ALU.mult, op1=ALU.mult)

        def dn_chain():
            order = (1, 0, 2)
            for b in order:
                for j in range(4 * b, 4 * b + 4):
                    bk = 1 + b
                    lo = (j % 4) * 128
                    for kc in range(8):
                        MM(bank(bk, lo, lo + 128), win[:, kc * IN_W + j * 128:kc * IN_W + (j + 1) * 128], hT[:, kc * 128:(kc + 1) * 128],
                           [T["hT"], T["win"]], [BK[bk]], start=(kc == 0), stop=(kc == 7), acc=(kc > 0 or j % 4 > 0),
                           signal=(kc == 7 and j % 4 == 3))
                if b != 0:
                    I("act", "activation", [BK[1 + b]], [T["cvb"]], acc=True, out=cv3[:, 4 * b:4 * b + 4, 3:131],
                      in_=bank(1 + b).rearrange("p (j t) -> p j t", t=128), func=AF.Copy)
                else:
                    I("dve", "tensor_copy", [BK[1 + b]], [T["cvb"]], acc=True, out=cv3[:, 4 * b:4 * b + 4, 3:131],
                      in_=bank(1 + b).rearrange("p (j t) -> p j t", t=128))
                yield
            for b in order:
                for j in range(4 * b, 4 * b + 4):
                    bk = 1 + b
                    lo = (j % 4) * 128
                    for i in range(4):
                        MM(bank(bk, lo, lo + 128), Dg[:, (i * 12 + j) * 128:(i * 12 + j + 1) * 128], cv3[:, j, i:i + 128],
                           [T["cvb"], T["Dg"]], [BK[bk]], start=(i == 0), stop=(i == 3), acc=(i > 0 or j % 4 > 0),
                           signal=(i == 3 and j % 4 == 3))
                yield
            I("pool", "tensor_copy", [T["cvb"]], [T["cvb"]], out=cv3[:, :, 0:3], in_=cv3[:, :, 128:131])
            for b in order:
                if b < 2:
                    dst, tdst = SB[:, b * 512:(b + 1) * 512], TSB[b]
                else:
                    dst, tdst = vTb[:], T["vTb"]
                sigmoid_chain(dst, bank(1 + b), 512, [BK[1 + b]], [tdst])
                I("dve", "tensor_tensor", [BK[1 + b], tdst], [tdst], out=dst, in0=bank(1 + b), in1=dst, op=ALU.mult)
                yield
            yield ("spawn", [gla_branch()])
            yield ("drain", carry)
            I("act", "activation", [TSB[1]], [T["sqb"]], acc=True, out=sqb[:, 512:1024], in_=SB[:, 512:1024], func=AF.Square)
            MM(bank(7), onesb[:], sqb[:, 512:1024], [T["sqb"], T["onesb"]], [BK[7]])
            I("act", "activation", [BK[7]], [TSA[1]], out=SA1, in_=bank(7), func=AF.Ln, bias=EPS)
            I("act", "activation", [TSA[1]], [TSA[1]], out=SA1, in_=SA1, func=AF.Exp, scale=-0.5)
            I("dve", "tensor_tensor", [TSB[1], TSA[1]], [T["qkn"]], acc=True, out=qkn[:, 512:1024], in0=SB[:, 512:1024], in1=SA1, op=ALU.mult)
            yield
            for h in range(4):
                kn = qkn[:, 512 + h * 128:512 + (h + 1) * 128]
                MM(bank(7, h * 128, (h + 1) * 128), kn, kn, [T["qkn"]], [BK[7]], acc=(h > 0), signal=(h == 3))
            I("act", "activation", [TSB[0]], [T["sqb"]], acc=True, out=sqb[:, 0:512], in_=SB[:, 0:512], func=AF.Square)
            MM(bank(5), onesb[:], sqb[:, 0:512], [T["sqb"], T["onesb"]], [BK[5]])
            yield
            assert flags["gate"], "gate branch must be fully emitted before the DN chain consumes it"
            for h in range(4):
                I("dve", "scalar_tensor_tensor", [BK[7], CT["bcol"], TSA[0]], [T["Xp"]], acc=True, out=Xp[:, h * 128:(h + 1) * 128],
                  in0=bank(7, h * 128, (h + 1) * 128), scalar=c_bcol[:, h:h + 1], in1=decI[:, h * 128:(h + 1) * 128],
                  op0=ALU.mult, op1=ALU.mult)
            I("dve", "tensor_tensor", [T["Xp"], T["mSU"]], [T["Xp"]], out=Xp[:], in0=Xp[:], in1=mSU[:], op=ALU.mult)
            yield
            for h in range(4):
                TP(bankb(0, h * 128, (h + 1) * 128), Xp[:, h * 128:(h + 1) * 128], identb[:], [T["Xp"], T["identb"]], [BK[0]],
                   acc=(h > 0), signal=(h == 3))
            I("act", "activation", [BK[0]], [T["Qm0"]], out=Qm[0][:], in_=bankb(0, 0, 512), func=AF.Copy)
            I("dve", "tensor_tensor", [T["I4b"], T["Xp"]], [T["Tm0"]], out=Tm[0][:], in0=I4b[:], in1=Xp[:], op=ALU.subtract)
            yield
            def filler(lv):
                if lv == 1:
                    I("act", "activation", [BK[5]], [TSA[1]], out=SA1, in_=bank(5), func=AF.Ln, bias=EPS)
                    I("act", "activation", [TSA[1]], [TSA[1]], out=SA1, in_=SA1, func=AF.Exp, scale=-0.5, bias=-0.5 * math.log(128.0))
                    I("dve", "tensor_tensor", [TSB[0], TSA[1]], [T["qkn"]], acc=True, out=qkn[:, 0:512], in0=SB[:, 0:512], in1=SA1, op=ALU.mult)
                elif lv == 2:
                    if not meta_tile:
                        for h in range(4):
                            kn = qkn[:, 512 + h * 128:512 + (h + 1) * 128]
                            MM(bank(5, h * 128, (h + 1) * 128), kn, qkn[:, h * 128:(h + 1) * 128], [T["qkn"]], [BK[5]], acc=(h > 0), signal=(h == 3))
                    for h in range(4):
                        TP(bankb(0, h * 128, (h + 1) * 128), qkn[:, 512 + h * 128:512 + (h + 1) * 128], identb[:], [T["qkn"], T["identb"]], [BK[0]],
                           acc=(h > 0), signal=False)
                    for h in range(4):
                        TP(bankb(0, 512 + h * 128, 512 + (h + 1) * 128), vTb[:, h * 128:(h + 1) * 128], identb[:], [T["vTb"], T["identb"]], [BK[0]],
                           acc=True, signal=(h == 3))
                    I("act", "activation", [BK[0]], [T["kvtm"]], out=kvtm[:], in_=bankb(0), func=AF.Copy)
                elif lv == 3:
                    if not meta_tile:
                        I("dve", "tensor_tensor", [BK[5], TSA[0]], [T["attnT"]], out=attnT[:], in0=bank(5), in1=decI, op=ALU.mult)
                    I("dve", "tensor_tensor", [T["qkn"], TSA[2]], [T["kgT"]], out=kgT[:], in0=qkn[:, 512:1024], in1=egc, op=ALU.mult)
                elif lv == 4:
                    for h in range(4):
                        s_ = slice(h * 128, (h + 1) * 128)
                        MM(bank(5, h * 128, (h + 1) * 128), kgT[:, s_], Sdb[:, s_], [T["kgT"], T["Sdb"]], [BK[5]], acc=(h > 0), signal=(h == 3))
                    I("dve", "tensor_tensor", [T["kvtm"], BK[5]], [T["rr"]], out=rr[:], in0=kvtm[:, 512:1024], in1=bank(5), op=ALU.subtract)
                    if not meta_tile:
                        I("dve", "tensor_tensor", [T["qkn"], TSA[2]], [T["qgT"]], out=qgT[:], in0=qkn[:, 0:512], in1=egc, op=ALU.mult)
                elif lv == 5:
                    for h in range(4):
                        I("dve", "tensor_scalar", [T["kvtm"], CT["dcol"]], [T["kd"]], acc=True, out=kd[:, h * 128:(h + 1) * 128],
                          in0=kvtm[:, h * 128:(h + 1) * 128], scalar1=c_dcol[:, h:h + 1], scalar2=None, op0=ALU.mult)

            Pc, Pt, Qc, Qt = Xp, T["Xp"], Qm[0], T["Qm0"]
            tstate = [Tm[0], T["Tm0"]]

            def t_update(lv, nQ, nQt):
                Tc, Tt = tstate
                nT, nTt = Tm[lv % 2], T["Tm%d" % (lv % 2)]
                for h in range(4):
                    s_ = slice(h * 128, (h + 1) * 128)
                    MM(bank(6, h * 128, (h + 1) * 128), nQ[:, s_], Tc[:, s_], [nQt, Tt], [BK[6]], acc=(h > 0), signal=(h == 3))
                I("dve", "tensor_tensor", [BK[6], Tt], [nTt], out=nT[:], in0=bank(6), in1=Tc[:], op=ALU.add)
                tstate[0], tstate[1] = nT, nTt

            pend = None
            for lv in range(1, 7):
                nP, nPt = Pm[lv % 2], T["Pm%d" % (lv % 2)]
                nQ, nQt = Qm[lv % 2], T["Qm%d" % (lv % 2)]
                for h in range(4):
                    s_ = slice(h * 128, (h + 1) * 128)
                    MM(bank(4, h * 128, (h + 1) * 128), Pc[:, s_], Qc[:, s_], [Pt, Qt], [BK[4]], acc=(h > 0), signal=(h == 3))
                if lv < 6:
                    for h in range(4):
                        s_ = slice(h * 128, (h + 1) * 128)
                        MM(bank(7, h * 128, (h + 1) * 128), Qc[:, s_], Pc[:, s_], [Pt, Qt], [BK[7]], acc=(h > 0), signal=(h == 3))
                I("act", "activation", [BK[4]], [nQt], out=nQ[:], in_=bank(4), func=AF.Copy)
                if lv < 6:
                    I("dve", "tensor_copy", [BK[7]], [nPt], out=nP[:], in_=bank(7))
                if pend is not None:
                    pend()
                pend = (lambda lv=lv, nQ=nQ, nQt=nQt: t_update(lv, nQ, nQt))
                filler(lv)
                Pc, Pt, Qc, Qt = nP, nPt, nQ, nQt
                if lv == 1:
                    yield ("spawn", [z_branch()])
                else:
                    yield
            pend()
            Tc, Tt = tstate
            yield
            for h in range(4):
                s_ = slice(h * 128, (h + 1) * 128)
                MM(bank(4, h * 128, (h + 1) * 128), Tc[:, s_], rr[:, s_], [Tt, T["rr"]], [BK[4]], acc=(h > 0), signal=(h == 3))
            for h in range(4):
                s_ = slice(h * 128, (h + 1) * 128)
                if h % 2 == 0:
                    I("act", "activation", [BK[4], CT["bcol"]], [T["vnew"]], acc=True, out=vnew[:, s_], in_=bank(4, h * 128, (h + 1) * 128),
                      func=AF.Copy, scale=c_bcol[:, h:h + 1])
                else:
                    I("dve", "tensor_scalar", [BK[4], CT["bcol"]], [T["vnew"]], acc=True, out=vnew[:, s_], in0=bank(4, h * 128, (h + 1) * 128),
                      scalar1=c_bcol[:, h:h + 1], scalar2=None, op0=ALU.mult)
            yield
            if not meta_tile:
                for h in range(4):
                    s_ = slice(h * 128, (h + 1) * 128)
                    MM(bank(6, h * 128, (h + 1) * 128), qgT[:, s_], Sdb[:, s_], [T["qgT"], T["Sdb"]], [BK[6]], start=True, stop=False,
                       acc=(h > 0), signal=False)
                    MM(bank(6, h * 128, (h + 1) * 128), attnT[:, s_], vnew[:, s_], [T["attnT"], T["vnew"]], [BK[6]], start=False, stop=True,
                       acc=True, signal=(h == 3))
            for h in range(4):
                s_ = slice(h * 128, (h + 1) * 128)
                MM(bank(7, h * 128, (h + 1) * 128), kd[:, s_], vnew[:, s_], [T["kd"], T["vnew"]], [BK[7]], acc=(h > 0), signal=(h == 3))
            yield
            if not meta_tile:
                assert flags["z"], "z branch must be emitted before the DN gate"
                norm_gate(6, 0, 0, c_ss4, c_rs4, CT["ss4"], CT["rs4"])
            for h in range(4):
                s_ = slice(h * 128, (h + 1) * 128)
                I("dve", "scalar_tensor_tensor", [BK[7], TSA[2], T["Sd"]], [T["Sd"]], acc=True, out=Sd[:, s_], in0=Sd[:, s_],
                  scalar=egc[:, h * 128 + 127:h * 128 + 128], in1=bank(7, h * 128, (h + 1) * 128), op0=ALU.mult, op1=ALU.add)
            I("act", "activation", [T["Sd"]], [T["Sdb"]], out=Sdb[:], in_=Sd[:], func=AF.Copy)
            yield

        run_parallel([dn_chain(), gate_branch()] + list(carry))
        if meta_tile:
            return None
        for h in range(4):
            I("act", "activation", [BK[1]], [T["ATm"], CT["ss5"]], acc=True, out=ATm[:, h * 128:(h + 1) * 128],
              in_=bank(1, h * 128, (h + 1) * 128), func=AF.Square, accum_out=c_ss5[:, h:h + 1])
        rsqrt_cols(c_ss5, c_rs5, CT["ss5"], CT["rs5"], 1.0 / 128)
        for h in range(4):
            I("dve", "scalar_tensor_tensor", [BK[1], CT["rs5"], TSB[1]], [T["mix"]], acc=True,
              out=mix[:, 512 + h * 128:512 + (h + 1) * 128], in0=bank(1, h * 128, (h + 1) * 128),
              scalar=c_rs5[:, h:h + 1], in1=SB[:, 512 + h * 128:512 + (h + 1) * 128], op0=ALU.mult, op1=ALU.mult)

        def tail():
            for kc in range(8):
                TP(bankb(4, kc * 128, (kc + 1) * 128), mix[:, kc * 128:(kc + 1) * 128], identb[:], [T["mix"], T["identb"]], [BK[4]],
                   acc=(kc > 0), signal=(kc == 7))
            I("act", "activation", [BK[4]], [T["mixT"]], out=mixT[:], in_=bankb(4), func=AF.Copy)
            yield
            x1t = T["X1_%d" % mslot]
            x1 = X1[:, mslot * 1024:(mslot + 1) * 1024]
            for half, bk in enumerate((7, 4)):
                for kc in range(8):
                    MM(bank(bk), mixT[:, kc * 128:(kc + 1) * 128], wout[:, kc * D + half * 512:kc * D + (half + 1) * 512],
                       [T["mixT"], T["wout"]], [BK[bk]], start=(kc == 0), stop=(kc == 7), acc=(kc > 0), signal=(kc == 7))
                yield
                I("dve", "tensor_tensor", [txt, BK[bk]], [x1t], acc=True, out=x1[:, half * 512:(half + 1) * 512],
                  in0=xt[:, half * 512:(half + 1) * 512], in1=bank(bk), op=ALU.add)
            yield
            I("act", "activation", [x1t], [T["mixT"], CT["ss2"]], out=mixT[:], in_=x1, func=AF.Square, accum_out=c_ss2)
            rsqrt_cols(c_ss2, c_rs2, CT["ss2"], CT["rs2"], 1.0 / D)
            yield
            I("dve", "tensor_scalar", [x1t, CT["rs2"]], [T["hb2"]], out=hb2[:], in0=x1, scalar1=c_rs2, scalar2=None, op0=ALU.mult)
            for kc in range(8):
                TP(bankb(7, kc * 128, (kc + 1) * 128), hb2[:, kc * 128:(kc + 1) * 128], identb[:], [T["hb2"], T["identb"]], [BK[7]],
                   acc=(kc > 0), signal=(kc == 7))
            yield
            I("act", "activation", [BK[7]], [T["H2T"]], acc=True,
              out=H2T.rearrange("p (k t) -> p k t", k=8)[:, :, mslot * 128:(mslot + 1) * 128],
              in_=bankb(7).rearrange("p (k t) -> p k t", k=8), func=AF.Copy)
            yield

        return tail()

    def mlp_macro(seq, tile0, nt, carry):
        N = nt * 128
        for g_ in carry:
            for _ in g_:
                pass
        H3 = H2T.rearrange("p (k t) -> p k t", k=8)

        def load(g):
            sl = g % 2
            DMA("wu", WU[sl], wup_s[g], [T["wup_s"]], [T["WU%d" % sl]])
            DMA("wd", WD[sl], wdn_s[g], [T["wdn_s"]], [T["WD%d" % sl]])

        def up(g):
            sl = g % 2
            for j in range(2):
                bk = 6 + j
                for kc in range(8):
                    MM(bank(bk, 0, N), WU[sl][:, kc * 256 + j * 128:kc * 256 + (j + 1) * 128], H3[:, kc, 0:N],
                       [T["WU%d" % sl], T["H2T"]], [BK[bk]], start=(kc == 0), stop=(kc == 7), acc=(kc > 0), signal=(kc == 7))
                I("act", "activation", [BK[bk]], [T["RL%d" % j]], out=RLs[j][:, 0:N], in_=bank(bk, 0, N), func=AF.Relu)
                I("dve", "tensor_tensor", [T["RL%d" % j]], [T["AT%d" % sl]], acc=(j > 0), out=ATb[sl][:, j * 384:j * 384 + N],
                  in0=RLs[j][:, 0:N], in1=RLs[j][:, 0:N], op=ALU.mult)

        def down(g):
            sl = g % 2
            for tt in range(nt):
                for half in range(2):
                    bk = tt * 2 + half
                    for j in range(2):
                        first = (g == 0 and j == 0)
                        last = (g == NGRP - 1 and j == 1)
                        MM(bank(bk), ATb[sl][:, j * 384 + tt * 128:j * 384 + (tt + 1) * 128],
                           WD[sl][:, j * 1024 + half * 512:j * 1024 + (half + 1) * 512], [T["AT%d" % sl], T["WD%d" % sl]], [BK[bk]],
                           start=first, stop=last, acc=(not first), signal=(last or (j == 1 and tt == nt - 1 and half == 1)))

        load(0)
        up(0)
        for g in range(NGRP):
            if g + 1 < NGRP:
                load(g + 1)
                up(g + 1)
            down(g)
        for tt in range(nt):
            x1t = T["X1_%d" % tt]
            x1 = X1[:, tt * 1024:(tt + 1) * 1024]
            for half in range(2):
                I("dve", "tensor_tensor", [x1t, BK[tt * 2 + half]], [x1t], acc=True, out=x1[:, half * 512:(half + 1) * 512],
                  in0=x1[:, half * 512:(half + 1) * 512], in1=bank(tt * 2 + half), op=ALU.add)

        def tail():
            for tt in range(nt):
                x1t = T["X1_%d" % tt]
                x1 = X1[:, tt * 1024:(tt + 1) * 1024]
                I("act", "activation", [x1t], [T["mixT"], CT["ss3"]], out=mixT[:], in_=x1, func=AF.Square, accum_out=c_ss3)
                rsqrt_cols(c_ss3, c_rs3, CT["ss3"], CT["rs3"], 1.0 / D)
                yield
                I("dve", "scalar_tensor_tensor", [x1t, CT["rs3"], T["gfbc"]], [x1t], out=x1, in0=x1, scalar=c_rs3, in1=gfbc[:],
                  op0=ALU.mult, op1=ALU.mult)
                tok0 = (tile0 + tt) * 128
                DMA("out", out_d[seq, tok0:tok0 + 128, :], x1, [x1t], [x1t])
                yield

        return tail()

    carry = []
    for seq in range(n_seq):
        mixer_tile(seq, 0, 0, carry)
        carry = []
        tile0 = 0
        for nt in macros:
            for tt in range(nt):
                tl = mixer_tile(seq, 1 + tile0 + tt, tt, carry)
                carry = [tl]
            tl = mlp_macro(seq, tile0, nt, carry)
            carry = [tl]
            tile0 += nt
    for g_ in carry:
        for _ in g_:
            pass
    I("pool", "memset", [T["X1_0"], T["X1_1"], T["X1_2"]], [CT["fin"]], ap=c_fin, constant=0.0)

    S.finalize()
    keys = list(S.ENG) + S.dmakeys
    sems = {k: es.enter_context(nc.semaphore("s_" + k)) for k in keys}
    block = es.enter_context(nc.Block())

    @block.tensor
    def _(eng):
        S.replay("pe", eng, sems)

    @block.scalar
    def _(eng):
        S.replay("act", eng, sems)

    @block.vector
    def _(eng):
        S.replay("dve", eng, sems)

    @block.gpsimd
    def _(eng):
        S.replay("pool", eng, sems)

    @block.sync
    def _(eng):
        S.replay("sp", eng, sems)

    es.close()
    S.sbtot = SBTOT[0]
    return nc, S


def make_consts():
    c = np.zeros((128, C_END), np.float32)
    s = np.arange(128)[:, None]
    cc = np.arange(128)[None, :]
    c[:, C_ID:C_ID + 128] = np.eye(128, dtype=np.float32)
    c[:, C_MUI:C_MUI + 512] = np.tile((cc >= s).astype(np.float32), (1, 4))
    c[:, C_MSU:C_MSU + 512] = np.tile((cc > s).astype(np.float32), (1, 4))
    c[:, C_NEG:C_NEG + 512] = np.tile(np.where(cc >= s, 0.0, -30000.0).astype(np.float32), (1, 4))
    for h in range(4):
        c[4 + h, C_SEL + h * 128:C_SEL + (h + 1) * 128] = 1.0
        c[4 + h, C_RM + h] = 1.0
    return c


def run(inputs, n_seq, macros, ncores):
    nc, S = build_program(n_seq, macros)
    n_real = sum(macros) * 128
    f = lambda a: np.ascontiguousarray(np.asarray(a, dtype=np.float32))
    x = f(inputs["x"])
    shared = {
        "meta": f(inputs["meta_tokens"]), "w_in": f(inputs["w_in"][0]), "w_out": f(inputs["w_out"][0]),
        "w_up": f(inputs["w_up"][0]), "w_down": f(inputs["w_down"][0]), "conv_w": f(inputs["conv_w"][0]),
        "consts": make_consts(), "norm1_g": f(inputs["norm1_g"][0]), "norm2_g": f(inputs["norm2_g"][0]),
        "final_norm_g": f(inputs["final_norm_g"]), "dn_norm_g": f(inputs["dn_norm_g"][0]),
        "gla_norm_g": f(inputs["gla_norm_g"][0]), "a_log": f(inputs["a_log"][0]), "dt_bias": f(inputs["dt_bias"][0]),
        "gla_w2": f(inputs["gla_w2"][0]), "gla_b": f(inputs["gla_b"][0]),
    }
    in_maps = []
    for c in range(ncores):
        m = dict(shared)
        m["x"] = np.ascontiguousarray(x[c * n_seq:(c + 1) * n_seq, :n_real])
        in_maps.append(m)
    res = run_bass_kernel_spmd(nc, in_maps, core_ids=list(range(ncores)))
    return np.concatenate([r["out"] for r in res.results], axis=0)


def kernel(**inputs):
    return run(inputs, 2, [3] * 10 + [2], NCORES)
```

```python
import math
from contextlib import ExitStack

import numpy as np
import concourse.bass as bass
import concourse.mybir as mybir
from concourse.bass_utils import run_bass_kernel_spmd

F32 = mybir.dt.float32
BF16 = mybir.dt.bfloat16
AF = mybir.ActivationFunctionType
ALU = mybir.AluOpType
AX = mybir.AxisListType

D = 1024
SEQ = 4096
NCORES = 8
IN_W = 3608
DFF = 4096
EPS = 1e-6
O_DQ, O_DK, O_DV, O_DZ, O_DB, O_GQ, O_GK, O_GV, O_GR, O_GLR = 0, 512, 1024, 1536, 2048, 2056, 2312, 2568, 3080, 3592
NGRP = 16
C_ID, C_MUI, C_MSU, C_NEG, C_SEL, C_RM, C_END = 0, 128, 640, 1152, 1664, 2176, 2180


class Trk:
    __slots__ = ("w", "rs", "psum")

    def __init__(self, psum=False):
        self.w = []
        self.rs = []
        self.psum = psum


SLACK = 0.6


class Sched:
    ENG = ("pe", "act", "dve", "pool", "sp")
    LIST_SCHED = True

    def __init__(self):
        self.ops = []
        self.segs = [0]
        self.dmakeys = []
        self.nins = 0

    def _eng(self, i):
        return self.ops[i][0]

    def _record(self, e, key, emit, dur, reads, writes, acc):
        i = len(self.ops)
        deps, order = set(), set()
        for t in reads:
            deps.update(t.w)
            if t.psum:
                for r in t.rs:
                    if self._eng(r) != e and self._eng(r) != "pe":
                        deps.add(r)
        for t in writes:
            for m in t.w:
                if acc and self._eng(m) == e and self.ops[m][1] is None:
                    order.add(m)
                else:
                    deps.add(m)
            deps.update(t.rs)
        self.ops.append([e, key, emit, dur, deps, order - deps])
        self.nins += 1
        for t in reads:
            t.rs.append(i)
        for t in writes:
            if acc:
                t.w = [m for m in t.w if self._eng(m) != e] + [i]
            else:
                t.w = [i]
            t.rs = []

    def op(self, e, emit, reads=(), writes=(), acc=False, signal=True, dur=0.3):
        self._record(e, None, emit, dur, reads, writes, acc)

    def dma(self, e, key, emit, reads=(), writes=(), dur=3.0):
        if key not in self.dmakeys:
            self.dmakeys.append(key)
        self._record(e, key, emit, dur, reads, writes, False)

    def barrier(self):
        self.segs.append(len(self.ops))

    def _schedule_segment(self, lo, hi, t0):
        import heapq
        ops = self.ops
        n = hi - lo
        succ = [[] for _ in range(n)]
        indeg = [0] * n
        for i in range(lo, hi):
            for p in ops[i][4] | ops[i][5]:
                if p >= lo:
                    succ[p - lo].append(i)
                    indeg[i - lo] += 1
        prio = [0.0] * n
        for i in range(hi - 1, lo - 1, -1):
            m = 0.0
            for sidx in succ[i - lo]:
                if prio[sidx - lo] > m:
                    m = prio[sidx - lo]
            prio[i - lo] = m + ops[i][3]
        ready_t = [t0] * n
        finish = {}
        efree = {e: t0 for e in self.ENG}
        wait_h = {e: [] for e in self.ENG}
        prio_h = {e: [] for e in self.ENG}
        order = {e: [] for e in self.ENG}
        glob = []
        for i in range(lo, hi):
            if indeg[i - lo] == 0:
                heapq.heappush(wait_h[ops[i][0]], (ready_t[i - lo], i))
        done = 0
        tmax = t0
        while done < n:
            best = None
            for e in self.ENG:
                wh, ph = wait_h[e], prio_h[e]
                while wh and wh[0][0] <= efree[e]:
                    _, i = heapq.heappop(wh)
                    heapq.heappush(ph, (-prio[i - lo], i))
                if ph and wh and wh[0][0] - efree[e] < SLACK and -prio[wh[0][1] - lo] < ph[0][0] - 1.0:
                    cand = (wh[0][0], 1, -prio[wh[0][1] - lo], e)
                elif ph:
                    cand = (efree[e], 0, ph[0][0], e)
                elif wh:
                    cand = (wh[0][0], 1, 0.0, e)
                else:
                    continue
                if best is None or cand < best:
                    best = cand
            start, kind, _, e = best
            if kind == 0:
                _, i = heapq.heappop(prio_h[e])
            else:
                _, i = heapq.heappop(wait_h[e])
            dur = ops[i][3]
            if ops[i][1] is not None:
                efree[e] = start + 0.07
            else:
                efree[e] = start + dur
            fin = start + dur
            finish[i] = fin
            tmax = max(tmax, fin)
            order[e].append(i)
            glob.append(i)
            done += 1
            for sidx in succ[i - lo]:
                k = sidx - lo
                lat = 0.04 if ops[sidx][0] == e else 0.13
                if fin + lat > ready_t[k]:
                    ready_t[k] = fin + lat
                indeg[k] -= 1
                if indeg[k] == 0:
                    heapq.heappush(wait_h[ops[sidx][0]], (ready_t[k], sidx))
        return order, tmax, glob

    def finalize(self):
        ops = self.ops
        bounds = self.segs + [len(ops)]
        eng_order = {e: [] for e in self.ENG}
        seg_end = {e: [] for e in self.ENG}
        glob_all = []
        t0 = 0.0
        for si in range(len(bounds) - 1):
            lo, hi = bounds[si], bounds[si + 1]
            if self.LIST_SCHED:
                order, t0, glob = self._schedule_segment(lo, hi, t0)
            else:
                order = {e: [i for i in range(lo, hi) if ops[i][0] == e] for e in self.ENG}
                glob = list(range(lo, hi))
            glob_all.extend(glob)
            for e in self.ENG:
                eng_order[e].extend(order[e])
                seg_end[e].append(len(eng_order[e]))
        pos = {}
        for e in self.ENG:
            for k_, i in enumerate(eng_order[e]):
                pos[i] = k_
        semof = lambda i: ops[i][1] if ops[i][1] is not None else ops[i][0]
        needed = [False] * len(ops)
        best_preds = [None] * len(ops)
        for i, o in enumerate(ops):
            best = {}
            for p in o[4]:
                if o[0] == "pe" and ops[p][0] == "pe" and ops[p][1] is None and o[1] is None:
                    continue
                k = semof(p)
                if k not in best or pos[p] > pos[best[k]]:
                    best[k] = p
            best_preds[i] = list(best.values())
        K_eng = {e: {} for e in self.ENG}
        know = {}
        for i in glob_all:
            o = ops[i]
            K = K_eng[o[0]]
            surv = []
            for p in sorted(best_preds[i], key=lambda p_: -pos[p_]):
                k = semof(p)
                if K.get(k, -1) >= pos[p]:
                    continue
                surv.append(p)
                K[k] = pos[p]
                for k2, v2 in know[p].items():
                    if K.get(k2, -1) < v2:
                        K[k2] = v2
            best_preds[i] = surv
            for p in surv:
                needed[p] = True
            kn = dict(K)
            kn[semof(i)] = pos[i]
            know[i] = kn
        del know
        last_in_seg = []
        for si in range(len(bounds) - 2):
            last = {}
            for e in self.ENG:
                lo_pos = seg_end[e][si - 1] if si > 0 else 0
                for i in eng_order[e][lo_pos:seg_end[e][si]]:
                    last[semof(i)] = i
            for i in last.values():
                needed[i] = True
            last_in_seg.append(last)
        cnt = {}
        count_of = {}
        for e in self.ENG:
            for i in eng_order[e]:
                if needed[i]:
                    k = semof(i)
                    cnt[k] = cnt.get(k, 0) + 1
                    count_of[i] = cnt[k]
        prog = {}
        for e in self.ENG:
            seen = {}
            out = []
            seg_i = 0
            for pos, i in enumerate(eng_order[e]):
                while seg_i < len(last_in_seg) and pos >= seg_end[e][seg_i]:
                    bw = []
                    for k, j in last_in_seg[seg_i].items():
                        if k != e and seen.get(k, 0) < count_of[j]:
                            seen[k] = count_of[j]
                            bw.append((k, count_of[j]))
                    if bw:
                        out.append((bw, None, None))
                    seg_i += 1
                need = {}
                for p in best_preds[i]:
                    k = semof(p)
                    if need.get(k, 0) < count_of[p]:
                        need[k] = count_of[p]
                waits = []
                for k, c in need.items():
                    if seen.get(k, 0) < c:
                        seen[k] = c
                        waits.append((k, c))
                out.append((waits, ops[i][2], semof(i) if needed[i] else None))
            prog[e] = out
        self.prog = prog
        return prog

    def replay(self, e, eng, sems):
        for waits, emit, sig in self.prog[e]:
            for en, c in waits:
                eng.wait_ge(sems[en], c * (1 if en in self.ENG else 16))
            if emit is None:
                continue
            ins = emit(eng)
            if sig is not None:
                ins.then_inc(sems[sig], 1 if sig in self.ENG else 16)


import os
DBG_STOP = int(os.environ.get('DBG_STOP', '1000000'))


def build_program(n_seq, macros, debug=False):
    n_real = sum(macros) * 128
    nc = bass.Bass("TRN2", target_bir_lowering=False)

    def din(name, shape):
        return nc.dram_tensor(name, list(shape), F32, kind="ExternalInput").ap()

    x_d = din("x", [n_seq, n_real, D])
    meta_d = din("meta", [16, D])
    win_d = din("w_in", [D, IN_W])
    wout_d = din("w_out", [D, D])
    wup_d = din("w_up", [D, DFF])
    wdn_d = din("w_down", [DFF, D])
    conv_d = din("conv_w", [4, 1536])
    cst_d = din("consts", [128, C_END])
    g1_d = din("norm1_g", [D])
    g2_d = din("norm2_g", [D])
    gf_d = din("final_norm_g", [D])
    gdn_d = din("dn_norm_g", [128])
    ggl_d = din("gla_norm_g", [128])
    alog_d = din("a_log", [4])
    dtb_d = din("dt_bias", [4])
    w2_d = din("gla_w2", [16, 256])
    gb_d = din("gla_b", [256])
    out_d = nc.dram_tensor("out", [n_seq, n_real, D], F32, kind="ExternalOutput").ap()
    wup_s = nc.dram_tensor("wup_s", [NGRP, 128, 8 * 256], BF16, kind="Internal").ap()
    wdn_s = nc.dram_tensor("wdn_s", [NGRP, 128, 2 * 1024], BF16, kind="Internal").ap()

    S = Sched()
    SBTOT = [0]
    es = ExitStack()
    TR = {}

    def sb(name, free, dt, parts=128):
        t = es.enter_context(nc.sbuf_tensor(name, [parts, free], dt))
        TR[name] = Trk()
        SBTOT[0] += free * (2 if dt == BF16 else 4)
        return t

    def _free(ap):
        n = 1
        for d in ap.shape[1:]:
            n *= int(d)
        return n

    def I(e, method, reads, writes, acc=False, signal=True, **kw):
        ap = kw.get("out", kw.get("ap"))
        n = _free(ap) if ap is not None else 128
        aps = [v for v in kw.values() if hasattr(v, "dtype") and hasattr(v, "shape")]
        all16 = all(v.dtype == BF16 for v in aps)
        if e == "act":
            dur = 0.22 + n / 1100.0
        elif e == "dve":
            rate = 1900.0 if all16 else 900.0
            dur = 0.12 + n / rate * (2.0 if method == "tensor_tensor_scan" else 1.0)
        else:
            dur = 0.3 + n / 150.0
        S.op(e, lambda eng, m=method, kw=kw: getattr(eng, m)(**kw), reads=reads, writes=writes, acc=acc, signal=signal, dur=dur)

    def MM(out, lhsT, rhs, reads, writes, start=True, stop=True, acc=False, signal=True):
        n = _free(out)
        dur = max(n, 90) / 2400.0 * (4.0 if lhsT.dtype == F32 else 1.0)
        S.op("pe", lambda eng, o=out, l=lhsT, r=rhs, st=start, sp=stop: eng.matmul(o, lhsT=l, rhs=r, start=st, stop=sp),
             reads=reads, writes=writes, acc=acc, signal=signal, dur=dur)

    def TP(out, in_, ident, reads, writes, acc=False, signal=True):
        S.op("pe", lambda eng, o=out, i=in_, d=ident: eng.transpose(o, i, d), reads=reads, writes=writes, acc=acc, signal=signal, dur=0.09)

    dkeys = {}

    def DMA(key, out, in_, reads, writes, e="sp", **kw):
        key = dkeys.setdefault(id(writes[0]), "k%d" % len(dkeys))
        nb = 128 * _free(out if len(out.shape) > 1 else in_) * 4
        S.dma(e, key, lambda eng, o=out, i=in_, kw=kw: eng.dma_start(out=o, in_=i, **kw), reads=reads, writes=writes,
              dur=2.0 + nb / 150e3)

    win = sb("win", 8 * IN_W, BF16)
    wout = sb("wout", 8 * D, BF16)
    Dg = sb("Dg", 48 * 128, BF16)
    identf = sb("identf", 128, F32)
    identb = sb("identb", 128, BF16)
    onesb = sb("onesb", 128, BF16)
    onesf = sb("onesf", 128, F32)
    I4b = sb("I4b", 512, BF16)
    mUI = sb("mUI", 512, BF16)
    mSU = sb("mSU", 512, BF16)
    gfbc = sb("gfbc", 1024, F32)
    w2a = sb("w2a", 256, F32)
    cols = sb("cols", 96, F32)
    XT = [sb("xt0", 1024, F32), sb("xt1", 1024, F32)]
    hb = sb("hb", 1024, BF16)
    hb2 = sb("hb2", 1024, BF16)
    hT = sb("hT", 1024, BF16)
    SA = sb("SA", 1536, F32)
    SB = sb("SB", 1024, F32)
    cvb = sb("cvb", 12 * 131, BF16)
    qkn = sb("qkn", 1024, BF16)
    sqb = sb("sqb", 1024, BF16)
    kvtm = sb("kvtm", 1024, BF16)
    vTb = sb("vTb", 512, BF16)
    g8 = sb("g8", 4 * 128, F32)
    gcol = sb("gcol", 16, F32)
    kgT = sb("kgT", 512, BF16)
    qgT = sb("qgT", 512, BF16)
    kd = sb("kd", 512, BF16)
    Xp = sb("Xp", 512, BF16)
    Pm = [sb("Pm%d" % i, 512, BF16) for i in range(2)]
    Qm = [sb("Qm%d" % i, 512, BF16) for i in range(2)]
    Tm = [sb("Tm%d" % i, 512, BF16) for i in range(2)]
    attnT = sb("attnT", 512, BF16)
    rr = sb("rr", 512, BF16)
    vnew = sb("vnew", 512, BF16)
    Sd = sb("Sd", 512, F32)
    Sdb = sb("Sdb", 512, BF16)
    glra = sb("glra", 128, F32)
    Lg = sb("Lg", 256, F32)
    Bc = sb("Bc", 256, F32)
    Eqi = sb("Eqi", 256, F32)
    Eki = sb("Eki", 256, F32)
    Ekd = sb("Ekd", 256, F32)
    Eqg = sb("Eqg", 256, F32)
    gqiT = sb("gqiT", 256, BF16)
    gkiT = [sb("gkiT%d" % r, 256, BF16) for r in range(2)]
    gqgT = [sb("gqgT%d" % r, 256, BF16) for r in range(2)]
    gkdT = sb("gkdT", 256, BF16)
    kdgtm = sb("kdgtm", 256, BF16)
    gvtm = sb("gvtm", 512, BF16)
    ATm = sb("ATm", 512, BF16)
    Sg = sb("Sg", 256, F32)
    Sgb = sb("Sgb", 256, BF16)
    mix = sb("mix", 1024, BF16)
    mixT = sb("mixT", 1024, BF16)
    U_X1, U_H2T, U_WU, U_WD, U_AT, U_RL = 0, 6144, 9216, 13312, 17408, 18944
    U_END = 19712
    U = sb("U", max(U_END, 16384), BF16)
    X1 = U[:, U_X1:U_X1 + 6144].bitcast(F32)
    H2T = U[:, U_H2T:U_H2T + 3072]
    WU = [U[:, U_WU + i * 2048:U_WU + (i + 1) * 2048] for i in range(2)]
    WD = [U[:, U_WD + i * 2048:U_WD + (i + 1) * 2048] for i in range(2)]
    ATb = [U[:, U_AT + i * 768:U_AT + (i + 1) * 768] for i in range(2)]
    RLs = [U[:, U_RL + i * 384:U_RL + (i + 1) * 384] for i in range(2)]
    stg_f = U[:, 0:8192].bitcast(F32)
    stg_b = [U[:, 8192 + i * 4096:8192 + (i + 1) * 4096] for i in range(2)]
    for nm in ["X1_0", "X1_1", "X1_2", "H2T", "WU0", "WU1", "WD0", "WD1", "AT0", "AT1", "RL0", "RL1", "stgf", "stgb0", "stgb1",
               "wup_s", "wdn_s", "SA0", "SA1", "SA2", "SB0", "SB1"]:
        TR[nm] = Trk()
    PS = es.enter_context(nc.psum_tensor("PS", [128, 4096], F32))
    PSb = PS.bitcast(BF16)
    BK = [Trk(psum=True) for _ in range(8)]

    def bank(b, lo=0, hi=512):
        return PS[:, b * 512 + lo:b * 512 + hi]

    def bankb(b, lo=0, hi=1024):
        return PSb[:, b * 1024 + lo:b * 1024 + hi]

    def col(i, n=1):
        return cols[:, i:i + n]
    CT = {k: Trk() for k in ["ss1", "rs1", "ss2", "rs2", "ss3", "rs3", "g1c", "g2c", "bcol", "dl4", "dcol", "ss4", "rs4",
                             "ss5", "rs5", "cA", "cB", "cC", "gl", "p8", "fin", "pm", "rm", "gnc", "gnc2"]}
    c_ss1, c_rs1, c_ss2, c_rs2, c_ss3, c_rs3 = col(0), col(1), col(2), col(3), col(4), col(5)
    c_g1, c_g2 = col(8, 8), col(16, 8)
    c_bcol, c_dl4, c_dcol, c_ss4, c_rs4, c_ss5, c_rs5 = col(24, 4), col(28, 4), col(32, 4), col(36, 4), col(40, 4), col(44, 4), col(48, 4)
    c_cA, c_cB, c_cC, c_gl = col(52, 2), col(54, 2), col(56, 2), col(58, 2)
    c_scl, c_bia, c_coef = col(60), col(61), col(62)
    c_fin = col(64)

    T = TR
    DMA("c0", stg_f[:, 0:C_END], cst_d, [], [T["stgf"]])
    I("dve", "tensor_copy", [T["stgf"]], [T["identf"]], out=identf[:], in_=stg_f[:, C_ID:C_ID + 128])
    I("dve", "tensor_copy", [T["stgf"]], [T["identb"]], out=identb[:], in_=stg_f[:, C_ID:C_ID + 128])
    I("dve", "tensor_copy", [T["stgf"]], [T["mUI"]], out=mUI[:], in_=stg_f[:, C_MUI:C_MUI + 512])
    I("dve", "tensor_copy", [T["stgf"]], [T["mSU"]], out=mSU[:], in_=stg_f[:, C_MSU:C_MSU + 512])
    I("dve", "tensor_copy", [T["stgf"]], [CT["rm"]], out=cols[0:8, 70:74], in_=stg_f[0:8, C_RM:C_RM + 4])
    for h in range(4):
        I("dve", "tensor_copy", [T["stgf"]], [T["I4b"]], out=I4b[:, h * 128:(h + 1) * 128], in_=stg_f[:, C_ID:C_ID + 128])
    I("pool", "memset", [], [T["onesb"]], ap=onesb[:], constant=1.0)
    I("pool", "memset", [], [T["onesf"]], ap=onesf[:], constant=1.0)
    I("pool", "memset", [], [T["glra"]], ap=glra[0:32, :], constant=1.0)
    I("pool", "memset", [], [CT["p8"]], ap=cols[:, 60:63], constant=0.0)
    I("pool", "memset", [], [CT["pm"]], ap=cols[:, 66:70], constant=0.0)
    I("pool", "memset", [CT["pm"]], [CT["pm"]], ap=cols[0:64, 66:67], constant=1.0)
    I("pool", "memset", [CT["pm"]], [CT["pm"]], ap=cols[64:128, 67:68], constant=1.0)
    I("pool", "memset", [CT["pm"]], [CT["pm"]], ap=cols[0:64, 68:69], constant=0.125)
    I("pool", "memset", [CT["pm"]], [CT["pm"]], ap=cols[64:128, 69:70], constant=0.125)
    DMA("c1", cols[:, 74:75], gdn_d.rearrange("(p o) -> p o", o=1), [], [CT["gnc"]], allow_slow_non_contiguous=True)
    DMA("c1", cols[:, 75:76], ggl_d.rearrange("(p o) -> p o", o=1), [], [CT["gnc2"]], allow_slow_non_contiguous=True)
    DMA("c1", gfbc[:], gf_d.partition_broadcast(128), [], [T["gfbc"]])
    DMA("c1", c_g1, g1_d.rearrange("(k p) -> p k", p=128), [], [CT["g1c"]], allow_slow_non_contiguous=True)
    DMA("c1", c_g2, g2_d.rearrange("(k p) -> p k", p=128), [], [CT["g2c"]], allow_slow_non_contiguous=True)
    DMA("c1", w2a[0:16, :], w2_d, [], [T["w2a"]])
    DMA("c1", w2a[16:17, :], gb_d.rearrange("(o n) -> o n", o=1), [], [T["w2a"]])
    DMA("c1", cols[4:8, 62:63], alog_d.rearrange("(p o) -> p o", o=1), [CT["p8"]], [CT["p8"]], allow_slow_non_contiguous=True)
    DMA("c1", cols[4:8, 61:62], dtb_d.rearrange("(p o) -> p o", o=1), [CT["p8"]], [CT["p8"]], allow_slow_non_contiguous=True)
    I("act", "activation", [CT["p8"]], [CT["p8"]], out=cols[0:8, 62:63], in_=cols[0:8, 62:63], func=AF.Exp)
    I("dve", "tensor_scalar", [CT["p8"]], [CT["p8"]], out=cols[0:8, 62:63], in0=cols[0:8, 62:63], scalar1=-1.0, scalar2=None, op0=ALU.mult)
    I("pool", "memset", [CT["p8"]], [CT["p8"]], ap=cols[0:8, 60:61], constant=1.0)
    I("pool", "memset", [CT["p8"]], [CT["p8"]], ap=cols[0:4, 60:61], constant=-1.0)
    DMA("c2", SB[:, 0:48].rearrange("p (i j) -> p i j", i=4), conv_d.rearrange("i (j p) -> p i j", p=128), [], [T["SB0"]],
        allow_slow_non_contiguous=True)
    for ij in range(48):
        I("dve", "tensor_scalar", [T["SB0"], T["identb"]], [T["Dg"]], acc=True,
          out=Dg[:, ij * 128:(ij + 1) * 128], in0=identb[:], scalar1=SB[:, ij:ij + 1], scalar2=None, op0=ALU.mult)
    win_v = win_d.rearrange("(k p) n -> k p n", p=128)
    for kc in range(8):
        DMA("c3", stg_f[:, 0:IN_W], win_v[kc], [], [T["stgf"]])
        I("dve", "tensor_scalar", [T["stgf"], CT["g1c"]], [T["win"]], acc=True,
          out=win[:, kc * IN_W:(kc + 1) * IN_W], in0=stg_f[:, 0:IN_W], scalar1=cols[:, 8 + kc:9 + kc], scalar2=None, op0=ALU.mult)
    wout_v = wout_d.rearrange("(k p) n -> k p n", p=128)
    for kc in range(8):
        DMA("c3", stg_f[:, 0:D], wout_v[kc], [], [T["stgf"]])
        I("dve", "tensor_scalar", [T["stgf"], CT["gnc"], CT["gnc2"]], [T["wout"]], acc=True,
          out=wout[:, kc * D:(kc + 1) * D], in0=stg_f[:, 0:D], scalar1=cols[:, 74 + kc // 4:75 + kc // 4], scalar2=None, op0=ALU.mult)
    wup_v = wup_d.rearrange("(k p) n -> k p n", p=128)
    for kc in range(8):
        sl = kc % 2
        DMA("c3", stg_f[:, 0:DFF], wup_v[kc], [], [T["stgf"]])
        I("dve", "tensor_scalar", [T["stgf"], CT["g2c"]], [T["stgb%d" % sl]],
          out=stg_b[sl][:, 0:DFF], in0=stg_f[:, 0:DFF], scalar1=cols[:, 16 + kc:17 + kc], scalar2=None, op0=ALU.mult)
        DMA("c4", wup_s[:, :, kc * 256:(kc + 1) * 256].rearrange("g p f -> p g f"),
            stg_b[sl][:, 0:DFF].rearrange("p (g f) -> p g f", f=256), [T["stgb%d" % sl]], [T["wup_s"]])
    wdn_v = wdn_d.rearrange("(g j p) n -> g p j n", p=128, j=2)
    for gg in range(8):
        sl = gg % 2
        for q in range(2):
            DMA("c3", stg_f[:, q * 2048:(q + 1) * 2048].rearrange("p (j n) -> p j n", j=2), wdn_v[gg * 2 + q], [], [T["stgf"]])
        I("dve", "tensor_copy", [T["stgf"]], [T["stgb%d" % sl]], out=stg_b[sl][:, 0:4096], in_=stg_f[:, 0:4096])
        for q in range(2):
            DMA("c4", wdn_s[gg * 2 + q], stg_b[sl][:, q * 2048:(q + 1) * 2048], [T["stgb%d" % sl]], [T["wdn_s"]])
    n_init = len(S.ops)
    for nm in ["X1_0", "X1_1", "X1_2", "H2T", "WU0", "WU1", "WD0", "WD1", "AT0", "AT1", "RL0", "RL1"]:
        T[nm].w = list(range(n_init))

    def rsqrt_cols(ss_ap, rs_ap, tss, trs, scale, parts=128):
        I("act", "activation", [tss], [trs], out=rs_ap, in_=ss_ap, func=AF.Ln, scale=scale, bias=EPS)
        I("act", "activation", [trs], [trs], out=rs_ap, in_=rs_ap, func=AF.Exp, scale=-0.5)

    def sigmoid_chain(dst, src, n, rd, wr):
        I("act", "activation", rd, wr, out=dst, in_=src, func=AF.Exp, scale=-1.0)
        I("act", "activation", wr, wr, out=dst, in_=dst, func=AF.Ln, bias=1.0)
        I("act", "activation", wr, wr, out=dst, in_=dst, func=AF.Exp, scale=-1.0)

    def proj_tm(bk, col0, ncols):
        for kc in range(8):
            MM(bank(bk, 0, ncols), hT[:, kc * 128:(kc + 1) * 128], win[:, kc * IN_W + col0:kc * IN_W + col0 + ncols],
               [T["hT"], T["win"]], [BK[bk]], start=(kc == 0), stop=(kc == 7), acc=(kc > 0), signal=(kc == 7))

    def run_parallel(gens):
        gens = list(gens)
        rounds = 0
        while gens:
            rounds += 1
            if rounds > DBG_STOP:
                return
            k = 0
            while k < len(gens):
                try:
                    r = next(gens[k])
                    if isinstance(r, tuple) and r[0] == "spawn":
                        gens.extend(r[1])
                    elif isinstance(r, tuple) and r[0] == "drain":
                        for g in r[1]:
                            for _ in g:
                                pass
                    k += 1
                except StopIteration:
                    gens.pop(k)

    SA0, SA1, SA2 = SA[:, 0:512], SA[:, 512:1024], SA[:, 1024:1536]
    TSA = [T["SA0"], T["SA1"], T["SA2"]]
    TSB = [T["SB0"], T["SB1"]]

    def mixer_tile(seq, ti, mslot, carry):
        meta_tile = (ti == 0)
        xt = XT[ti % 2]
        txt = T["xt%d" % (ti % 2)]
        flags = {"gate": False, "z": False}
        if meta_tile:
            I("pool", "memset", [], [txt], ap=xt[:], constant=0.0)
            DMA("x", xt[112:128, :], meta_d, [], [txt])
            I("pool", "memset", [], [T["cvb"]], ap=cvb[:], constant=0.0)
            I("pool", "memset", [], [T["Sd"]], ap=Sd[:], constant=0.0)
            I("pool", "memset", [], [T["Sdb"]], ap=Sdb[:], constant=0.0)
            I("pool", "memset", [], [T["Sg"]], ap=Sg[:], constant=0.0)
            I("pool", "memset", [], [T["Sgb"]], ap=Sgb[:], constant=0.0)
        else:
            DMA("x", xt[:], x_d[seq, (ti - 1) * 128:ti * 128, :], [], [txt])
        I("act", "activation", [txt], [TSA[0], TSA[1], CT["ss1"]], out=SA[:, 0:1024], in_=xt[:], func=AF.Square, accum_out=c_ss1)
        rsqrt_cols(c_ss1, c_rs1, CT["ss1"], CT["rs1"], 1.0 / D)
        I("dve", "tensor_scalar", [txt, CT["rs1"]], [T["hb"]], out=hb[:], in0=xt[:], scalar1=c_rs1, scalar2=None, op0=ALU.mult)
        for kc in range(8):
            TP(bankb(0, kc * 128, (kc + 1) * 128), hb[:, kc * 128:(kc + 1) * 128], identb[:], [T["hb"], T["identb"]], [BK[0]],
               acc=(kc > 0), signal=(kc == 7))
        I("act", "activation", [BK[0]], [T["hT"]], out=hT[:], in_=bankb(0), func=AF.Copy)
        cv3 = cvb[:].rearrange("p (j t) -> p j t", t=131)
        decI, egc = SA0, SA2

        def gate_branch():
            for kc in range(8):
                MM(bank(5, 0, 128)[0:8, :], win[:, kc * IN_W + O_DB:kc * IN_W + O_DB + 8], hT[:, kc * 128:(kc + 1) * 128],
                   [T["hT"], T["win"]], [BK[5]], start=(kc == 0), stop=(kc == 7), acc=(kc > 0), signal=(kc == 7))
            E8, G8, GC8 = g8[0:8, 0:128], g8[0:8, 128:256], g8[0:8, 256:384]
            I("act", "activation", [BK[5], CT["p8"]], [T["g8"]], out=E8, in_=bank(5, 0, 128)[0:8, :], func=AF.Exp,
              scale=cols[0:8, 60:61], bias=cols[0:8, 61:62])
            yield
            I("act", "activation", [T["g8"]], [T["g8"]], out=E8, in_=E8, func=AF.Ln, bias=1.0)
            I("dve", "tensor_scalar", [T["g8"], CT["p8"]], [T["g8"]], out=G8, in0=E8, scalar1=cols[0:8, 62:63], scalar2=None, op0=ALU.mult)
            if meta_tile:
                I("dve", "memset", [T["g8"]], [T["g8"]], ap=g8[0:8, 128:128 + 112], constant=0.0)
            I("dve", "tensor_tensor_scan", [T["g8"], T["onesf"]], [T["g8"]], out=GC8, data0=onesf[0:8, :], data1=G8, initial=0.0,
              op0=ALU.mult, op1=ALU.add)
            yield
            TP(bank(5, 128, 136), G8, identf[0:8, 0:8], [T["g8"], T["identf"]], [BK[5]], acc=True, signal=False)
            TP(bank(5, 136, 144), GC8, identf[0:8, 0:8], [T["g8"], T["identf"]], [BK[5]], acc=True)
            for h in range(4):
                I("dve", "tensor_scalar", [T["g8"], CT["rm"]], [TSA[2]], acc=True, out=SA[0:8, 1024 + h * 128:1024 + (h + 1) * 128], in0=GC8,
                  scalar1=cols[0:8, 70 + h:71 + h], scalar2=None, op0=ALU.mult)
            MM(bank(6), onesf[0:8, 0:128], SA[0:8, 1024:1536], [TSA[2], T["onesf"]], [BK[6]])
            I("dve", "tensor_copy", [BK[5]], [T["gcol"]], out=gcol[:], in_=bank(5, 128, 144))
            I("act", "activation", [T["gcol"]], [CT["bcol"]], out=c_bcol, in_=gcol[:, 0:4], func=AF.Exp)
            yield
            for h in range(4):
                I("dve", "tensor_scalar", [BK[6], T["gcol"]], [TSA[0]], acc=True, out=SA[:, h * 128:(h + 1) * 128],
                  in0=bank(6, h * 128, (h + 1) * 128), scalar1=gcol[:, 12 + h:13 + h], scalar2=0.0, op0=ALU.subtract, op1=ALU.min)
            I("act", "activation", [BK[6], TSA[0]], [TSA[2]], out=egc, in_=bank(6), func=AF.Exp)
            yield
            I("act", "activation", [TSA[0]], [TSA[0]], out=decI, in_=decI, func=AF.Exp)
            I("dve", "tensor_tensor", [T["gcol"], BK[6], TSA[2]], [CT["dl4"]], out=c_dl4, in0=gcol[:, 12:16],
              in1=bank(6).rearrange("p (h c) -> p h c", c=128)[:, :, 127], op=ALU.subtract)
            I("act", "activation", [CT["dl4"]], [CT["dcol"]], out=c_dcol, in_=c_dl4, func=AF.Exp, scale=-1.0)
            I("dve", "tensor_tensor", [TSA[0], T["mUI"]], [TSA[0]], out=decI, in0=decI, in1=mUI[:], op=ALU.mult)
            flags["gate"] = True
            yield

        def gla_branch():
            for c in range(4):
                col0 = O_GQ + c * 128
                for kc in range(8):
                    MM(bank(1, c * 128, (c + 1) * 128), win[:, kc * IN_W + col0:kc * IN_W + col0 + 128], hT[:, kc * 128:(kc + 1) * 128],
                       [T["hT"], T["win"]], [BK[1]], start=(kc == 0), stop=(kc == 7), acc=(kc > 0 or c > 0), signal=(kc == 7 and c == 3))
                if c % 2 == 1:
                    yield
            for kc in range(8):
                MM(bank(2, 256, 384)[0:16, :], win[:, kc * IN_W + O_GLR:kc * IN_W + O_GLR + 16], hT[:, kc * 128:(kc + 1) * 128],
                   [T["hT"], T["win"]], [BK[2]], start=(kc == 0), stop=(kc == 7), acc=(kc > 0), signal=(kc == 7))
            I("act", "activation", [BK[2]], [T["glra"]], out=glra[0:16, :], in_=bank(2, 256, 384)[0:16, :], func=AF.Copy)
            yield
            for c in range(2):
                MM(bank(2, c * 128, (c + 1) * 128), w2a[0:17, c * 128:(c + 1) * 128], glra[0:17, :], [T["w2a"], T["glra"]], [BK[2]],
                   acc=True, signal=(c == 1))
            I("act", "activation", [BK[2]], [T["Lg"]], out=Lg[:], in_=bank(2, 0, 256), func=AF.Exp, scale=-1.0)
            yield
            I("act", "activation", [T["Lg"]], [T["Lg"]], out=Lg[:], in_=Lg[:], func=AF.Ln, bias=1.0)
            if meta_tile:
                I("dve", "memset", [T["Lg"]], [T["Lg"]], ap=Lg[:].rearrange("p (c t) -> p c t", c=2)[:, :, 0:112], constant=0.0)
            for c in range(2):
                I("dve", "tensor_tensor_scan", [T["Lg"], T["onesf"]], [T["Bc"]], acc=True, out=Bc[:, c * 128:(c + 1) * 128], data0=onesf[:],
                  data1=Lg[:, c * 128:(c + 1) * 128], initial=0.0, op0=ALU.mult, op1=ALU.add)
            yield
            Bc3 = Bc[:].rearrange("p (c t) -> p c t", c=2)
            I("dve", "tensor_scalar", [T["Bc"]], [CT["cA"]], out=c_cA, in0=Bc3[:, :, 64], scalar1=1.0 / 16, scalar2=None, op0=ALU.mult)
            I("dve", "tensor_scalar", [T["Bc"]], [CT["cB"]], out=c_cB, in0=Bc3[:, :, 64], scalar1=-1.0 / 16, scalar2=None, op0=ALU.mult)
            I("dve", "tensor_scalar", [T["Bc"]], [CT["cC"]], out=c_cC, in0=Bc3[:, :, 127], scalar1=-1.0 / 16, scalar2=None, op0=ALU.mult)
            yield
            for c in range(2):
                s_ = slice(c * 128, (c + 1) * 128)
                if not meta_tile:
                    I("act", "activation", [T["Bc"], CT["cA"]], [T["Eqi"]], acc=True, out=Eqi[:, s_], in_=Bc[:, s_], func=AF.Exp,
                      scale=-1.0 / 16, bias=c_cA[:, c:c + 1])
                    I("act", "activation", [T["Bc"], CT["cB"]], [T["Eki"]], acc=True, out=Eki[:, s_], in_=Bc[:, s_], func=AF.Exp,
                      scale=1.0 / 16, bias=c_cB[:, c:c + 1])
                I("act", "activation", [T["Bc"], CT["cC"]], [T["Ekd"]], acc=True, out=Ekd[:, s_], in_=Bc[:, s_], func=AF.Exp,
                  scale=1.0 / 16, bias=c_cC[:, c:c + 1])
                yield
            I("act", "activation", [CT["cC"]], [CT["gl"]], out=c_gl, in_=c_cC, func=AF.Exp)
            if not meta_tile:
                I("act", "activation", [T["Bc"]], [T["Eqg"]], out=Eqg[:], in_=Bc[:], func=AF.Exp, scale=-1.0 / 16)
                I("dve", "scalar_tensor_tensor", [BK[1], T["Eqi"]], [T["gqiT"]], out=gqiT[:], in0=bank(1, 0, 256), scalar=0.125, in1=Eqi[:],
                  op0=ALU.mult, op1=ALU.mult)
                yield
                for r in range(2):
                    I("dve", "scalar_tensor_tensor", [BK[1], T["Eki"], CT["pm"]], [T["gkiT%d" % r]], out=gkiT[r][:], in0=bank(1, 256, 512),
                      scalar=cols[:, 66 + r:67 + r], in1=Eki[:], op0=ALU.mult, op1=ALU.mult)
                yield
                for r in range(2):
                    I("dve", "scalar_tensor_tensor", [BK[1], T["Eqg"], CT["pm"]], [T["gqgT%d" % r]], out=gqgT[r][:], in0=bank(1, 0, 256),
                      scalar=cols[:, 68 + r:69 + r], in1=Eqg[:], op0=ALU.mult, op1=ALU.mult)
            I("dve", "tensor_tensor", [BK[1], T["Ekd"]], [T["gkdT"]], out=gkdT[:], in0=bank(1, 256, 512), in1=Ekd[:], op=ALU.mult)
            yield
            for c in range(2):
                TP(bankb(2, 768 + c * 128, 768 + (c + 1) * 128), gkdT[:, c * 128:(c + 1) * 128], identb[:], [T["gkdT"], T["identb"]], [BK[2]],
                   acc=True, signal=(c == 1))
            I("act", "activation", [BK[2]], [T["kdgtm"]], out=kdgtm[:], in_=bankb(2, 768, 1024), func=AF.Copy)
            yield
            proj_tm(2, O_GV, 512)
            I("act", "activation", [BK[2]], [T["gvtm"]], out=gvtm[:], in_=bank(2), func=AF.Copy)
            yield
            if not meta_tile:
                for h in range(4):
                    c, r = h // 2, h % 2
                    MM(bank(2, h * 128, (h + 1) * 128), gkiT[r][:, c * 128:(c + 1) * 128],
                       gqiT[:, c * 128:(c + 1) * 128], [T["gkiT%d" % r], T["gqiT"]], [BK[2]], acc=(h > 0), signal=(h == 3))
                I("dve", "tensor_tensor", [BK[2], T["mUI"]], [T["ATm"]], out=ATm[:], in0=bank(2), in1=mUI[:], op=ALU.mult)
                yield
                for h in range(4):
                    c, r = h // 2, h % 2
                    s_ = slice(h * 128, (h + 1) * 128)
                    MM(bank(1, h * 128, (h + 1) * 128), ATm[:, s_], gvtm[:, s_], [T["ATm"], T["gvtm"]], [BK[1]], start=True, stop=False,
                       acc=(h > 0), signal=False)
                    MM(bank(1, h * 128, (h + 1) * 128), gqgT[r][:, c * 128:(c + 1) * 128],
                       Sgb[:, c * 128:(c + 1) * 128], [T["gqgT%d" % r], T["Sgb"]], [BK[1]], start=False, stop=True,
                       acc=True, signal=(h == 3))
                yield
            for c in range(2):
                MM(bank(2, c * 256, (c + 1) * 256), kdgtm[:, c * 128:(c + 1) * 128], gvtm[:, c * 256:(c + 1) * 256],
                   [T["kdgtm"], T["gvtm"]], [BK[2]], acc=(c > 0), signal=(c == 1))
            yield
            for c in range(2):
                for r in range(2):
                    pr = slice(r * 64, (r + 1) * 64)
                    I("dve", "scalar_tensor_tensor", [BK[2], CT["gl"], T["Sg"]], [T["Sg"]], acc=True, out=Sg[pr, c * 128:(c + 1) * 128],
                      in0=Sg[pr, c * 128:(c + 1) * 128], scalar=cols[pr, 58 + c:59 + c],
                      in1=PS[pr, 2 * 512 + c * 256 + r * 128:2 * 512 + c * 256 + (r + 1) * 128], op0=ALU.mult, op1=ALU.add)
            I("pool", "tensor_copy", [T["Sg"]], [T["Sgb"]], out=Sgb[:], in_=Sg[:])
            yield

        def z_branch():
            for zi, zcol0 in enumerate((O_DZ, O_GR)):
                dst = SB[:, zi * 512:(zi + 1) * 512]
                proj_tm(3, zcol0, 512)
                yield
                sigmoid_chain(dst, bank(3), 512, [BK[3]], [TSB[zi]])
                yield
                I("dve", "tensor_tensor", [BK[3], TSB[zi]], [TSB[zi]], out=dst, in0=bank(3), in1=dst, op=ALU.mult)
                yield
            flags["z"] = True

        def norm_gate(obk, zi, mixoff, c_ss, c_rs, tss, trs):
            I("act", "activation", [BK[obk]], [TSA[1]], out=SA1, in_=bank(obk), func=AF.Square)
            I("dve", "tensor_reduce", [TSA[1]], [tss], out=c_ss, in_=SA1.rearrange("p (h e) -> p h e", h=4), axis=AX.X, op=ALU.add)
            rsqrt_cols(c_ss, c_rs, tss, trs, 1.0 / 128)
            for h in range(4):
                I("dve", "scalar_tensor_tensor", [BK[obk], trs, TSB[zi]], [T["mix"]], acc=True,
                  out=mix[:, mixoff + h * 128:mixoff + (h + 1) * 128], in0=bank(obk, h * 128, (h + 1) * 128),
                  scalar=c_rs[:, h:h + 1], in1=SB[:, zi * 512 + h * 128:zi * 512 + (h + 1) * 128], op0=ALU.mult, op1=ALU.mult)

        def dn_chain():
            order = (1, 0, 2)
            for b in order:
                for j in range(4 * b, 4 * b + 4):
                    bk = 1 + b
                    lo = (j % 4) * 128
                    for kc in range(8):
                        MM(bank(bk, lo, lo + 128), win[:, kc * IN_W + j * 128:kc * IN_W + (j + 1) * 128], hT[:, kc * 128:(kc + 1) * 128],
                           [T["hT"], T["win"]], [BK[bk]], start=(kc == 0), stop=(kc == 7), acc=(kc > 0 or j % 4 > 0),
                           signal=(kc == 7 and j % 4 == 3))
                if b != 0:
                    I("act", "activation", [BK[1 + b]], [T["cvb"]], acc=True, out=cv3[:, 4 * b:4 * b + 4, 3:131],
                      in_=bank(1 + b).rearrange("p (j t) -> p j t", t=128), func=AF.Copy)
                else:
                    I("dve", "tensor_copy", [BK[1 + b]], [T["cvb"]], acc=True, out=cv3[:, 4 * b:4 * b + 4, 3:131],
                      in_=bank(1 + b).rearrange("p (j t) -> p j t", t=128))
                yield
            for b in order:
                for j in range(4 * b, 4 * b + 4):
                    bk = 1 + b
                    lo = (j % 4) * 128
                    for i in range(4):
                        MM(bank(bk, lo, lo + 128), Dg[:, (i * 12 + j) * 128:(i * 12 + j + 1) * 128], cv3[:, j, i:i + 128],
                           [T["cvb"], T["Dg"]], [BK[bk]], start=(i == 0), stop=(i == 3), acc=(i > 0 or j % 4 > 0),
                           signal=(i == 3 and j % 4 == 3))
                yield
            I("pool", "tensor_copy", [T["cvb"]], [T["cvb"]], out=cv3[:, :, 0:3], in_=cv3[:, :, 128:131])
            for b in order:
                if b < 2:
                    dst, tdst = SB[:, b * 512:(b + 1) * 512], TSB[b]
                else:
                    dst, tdst = vTb[:], T["vTb"]
                sigmoid_chain(dst, bank(1 + b), 512, [BK[1 + b]], [tdst])
                I("dve", "tensor_tensor", [BK[1 + b], tdst], [tdst], out=dst, in0=bank(1 + b), in1=dst, op=ALU.mult)
                yield
            yield ("spawn", [gla_branch()])
            yield ("drain", carry)
            I("act", "activation", [TSB[1]], [T["sqb"]], acc=True, out=sqb[:, 512:1024], in_=SB[:, 512:1024], func=AF.Square)
            MM(bank(7), onesb[:], sqb[:, 512:1024], [T["sqb"], T["onesb"]], [BK[7]])
            I("act", "activation", [BK[7]], [TSA[1]], out=SA1, in_=bank(7), func=AF.Ln, bias=EPS)
            I("act", "activation", [TSA[1]], [TSA[1]], out=SA1, in_=SA1, func=AF.Exp, scale=-0.5)
            I("dve", "tensor_tensor", [TSB[1], TSA[1]], [T["qkn"]], acc=True, out=qkn[:, 512:1024], in0=SB[:, 512:1024], in1=SA1, op=ALU.mult)
            yield
            for h in range(4):
                kn = qkn[:, 512 + h * 128:512 + (h + 1) * 128]
                MM(bank(7, h * 128, (h + 1) * 128), kn, kn, [T["qkn"]], [BK[7]], acc=(h > 0), signal=(h == 3))
            I("act", "activation", [TSB[0]], [T["sqb"]], acc=True, out=sqb[:, 0:512], in_=SB[:, 0:512], func=AF.Square)
            MM(bank(5), onesb[:], sqb[:, 0:512], [T["sqb"], T["onesb"]], [BK[5]])
            yield
            assert flags["gate"], "gate branch must be fully emitted before the DN chain consumes it"
            for h in range(4):
                I("dve", "scalar_tensor_tensor", [BK[7], CT["bcol"], TSA[0]], [T["Xp"]], acc=True, out=Xp[:, h * 128:(h + 1) * 128],
                  in0=bank(7, h * 128, (h + 1) * 128), scalar=c_bcol[:, h:h + 1], in1=decI[:, h * 128:(h + 1) * 128],
                  op0=ALU.mult, op1=ALU.mult)
            I("dve", "tensor_tensor", [T["Xp"], T["mSU"]], [T["Xp"]], out=Xp[:], in0=Xp[:], in1=mSU[:], op=ALU.mult)
            yield
            for h in range(4):
                TP(bankb(0, h * 128, (h + 1) * 128), Xp[:, h * 128:(h + 1) * 128], identb[:], [T["Xp"], T["identb"]], [BK[0]],
                   acc=(h > 0), signal=(h == 3))
            I("act", "activation", [BK[0]], [T["Qm0"]], out=Qm[0][:], in_=bankb(0, 0, 512), func=AF.Copy)
            I("dve", "tensor_tensor", [T["I4b"], T["Xp"]], [T["Tm0"]], out=Tm[0][:], in0=I4b[:], in1=Xp[:], op=ALU.subtract)
            yield
            def filler(lv):
                if lv == 1:
                    I("act", "activation", [BK[5]], [TSA[1]], out=SA1, in_=bank(5), func=AF.Ln, bias=EPS)
                    I("act", "activation", [TSA[1]], [TSA[1]], out=SA1, in_=SA1, func=AF.Exp, scale=-0.5, bias=-0.5 * math.log(128.0))
                    I("dve", "tensor_tensor", [TSB[0], TSA[1]], [T["qkn"]], acc=True, out=qkn[:, 0:512], in0=SB[:, 0:512], in1=SA1, op=ALU.mult)
                elif lv == 2:
                    if not meta_tile:
                        for h in range(4):
                            kn = qkn[:, 512 + h * 128:512 + (h + 1) * 128]
                            MM(bank(5, h * 128, (h + 1) * 128), kn, qkn[:, h * 128:(h + 1) * 128], [T["qkn"]], [BK[5]], acc=(h > 0), signal=(h == 3))
                    for h in range(4):
                        TP(bankb(0, h * 128, (h + 1) * 128), qkn[:, 512 + h * 128:512 + (h + 1) * 128], identb[:], [T["qkn"], T["identb"]], [BK[0]],
                           acc=(h > 0), signal=False)
                    for h in range(4):
                        TP(bankb(0, 512 + h * 128, 512 + (h + 1) * 128), vTb[:, h * 128:(h + 1) * 128], identb[:], [T["vTb"], T["identb"]], [BK[0]],
                           acc=True, signal=(h == 3))
                    I("act", "activation", [BK[0]], [T["kvtm"]], out=kvtm[:], in_=bankb(0), func=AF.Copy)
                elif lv == 3:
                    if not meta_tile:
                        I("dve", "tensor_tensor", [BK[5], TSA[0]], [T["attnT"]], out=attnT[:], in0=bank(5), in1=decI, op=ALU.mult)
                    I("dve", "tensor_tensor", [T["qkn"], TSA[2]], [T["kgT"]], out=kgT[:], in0=qkn[:, 512:1024], in1=egc, op=ALU.mult)
                elif lv == 4:
                    for h in range(4):
                        s_ = slice(h * 128, (h + 1) * 128)
                        MM(bank(5, h * 128, (h + 1) * 128), kgT[:, s_], Sdb[:, s_], [T["kgT"], T["Sdb"]], [BK[5]], acc=(h > 0), signal=(h == 3))
                    I("dve", "tensor_tensor", [T["kvtm"], BK[5]], [T["rr"]], out=rr[:], in0=kvtm[:, 512:1024], in1=bank(5), op=ALU.subtract)
                    if not meta_tile:
                        I("dve", "tensor_tensor", [T["qkn"], TSA[2]], [T["qgT"]], out=qgT[:], in0=qkn[:, 0:512], in1=egc, op=ALU.mult)
                elif lv == 5:
                    for h in range(4):
                        I("dve", "tensor_scalar", [T["kvtm"], CT["dcol"]], [T["kd"]], acc=True, out=kd[:, h * 128:(h + 1) * 128],
                          in0=kvtm[:, h * 128:(h + 1) * 128], scalar1=c_dcol[:, h:h + 1], scalar2=None, op0=ALU.mult)

            Pc, Pt, Qc, Qt = Xp, T["Xp"], Qm[0], T["Qm0"]
            tstate = [Tm[0], T["Tm0"]]

            def t_update(lv, nQ, nQt):
                Tc, Tt = tstate
                nT, nTt = Tm[lv % 2], T["Tm%d" % (lv % 2)]
                for h in range(4):
                    s_ = slice(h * 128, (h + 1) * 128)
                    MM(bank(6, h * 128, (h + 1) * 128), nQ[:, s_], Tc[:, s_], [nQt, Tt], [BK[6]], acc=(h > 0), signal=(h == 3))
                I("dve", "tensor_tensor", [BK[6], Tt], [nTt], out=nT[:], in0=bank(6), in1=Tc[:], op=ALU.add)
                tstate[0], tstate[1] = nT, nTt

            pend = None
            for lv in range(1, 7):
                nP, nPt = Pm[lv % 2], T["Pm%d" % (lv % 2)]
                nQ, nQt = Qm[lv % 2], T["Qm%d" % (lv % 2)]
                for h in range(4):
                    s_ = slice(h * 128, (h + 1) * 128)
                    MM(bank(4, h * 128, (h + 1) * 128), Pc[:, s_], Qc[:, s_], [Pt, Qt], [BK[4]], acc=(h > 0), signal=(h == 3))
                if lv < 6:
                    for h in range(4):
                        s_ = slice(h * 128, (h + 1) * 128)
                        MM(bank(7, h * 128, (h + 1) * 128), Qc[:, s_], Pc[:, s_], [Pt, Qt], [BK[7]], acc=(h > 0), signal=(h == 3))
                I("act", "activation", [BK[4]], [nQt], out=nQ[:], in_=bank(4), func=AF.Copy)
                if lv < 6:
                    I("dve", "tensor_copy", [BK[7]], [nPt], out=nP[:], in_=bank(7))
                if pend is not None:
                    pend()
                pend = (lambda lv=lv, nQ=nQ, nQt=nQt: t_update(lv, nQ, nQt))
                filler(lv)
                Pc, Pt, Qc, Qt = nP, nPt, nQ, nQt
                if lv == 1:
                    yield ("spawn", [z_branch()])
                else:
                    yield
            pend()
            Tc, Tt = tstate
            yield
            for h in range(4):
                s_ = slice(h * 128, (h + 1) * 128)
                MM(bank(4, h * 128, (h + 1) * 128), Tc[:, s_], rr[:, s_], [Tt, T["rr"]], [BK[4]], acc=(h > 0), signal=(h == 3))
            for h in range(4):
                s_ = slice(h * 128, (h + 1) * 128)
                if h % 2 == 0:
                    I("act", "activation", [BK[4], CT["bcol"]], [T["vnew"]], acc=True, out=vnew[:, s_], in_=bank(4, h * 128, (h + 1) * 128),
                      func=AF.Copy, scale=c_bcol[:, h:h + 1])
                else:
                    I("dve", "tensor_scalar", [BK[4], CT["bcol"]], [T["vnew"]], acc=True, out=vnew[:, s_], in0=bank(4, h * 128, (h + 1) * 128),
                      scalar1=c_bcol[:, h:h + 1], scalar2=None, op0=ALU.mult)
            yield
            if not meta_tile:
                for h in range(4):
                    s_ = slice(h * 128, (h + 1) * 128)
                    MM(bank(6, h * 128, (h + 1) * 128), qgT[:, s_], Sdb[:, s_], [T["qgT"], T["Sdb"]], [BK[6]], start=True, stop=False,
                       acc=(h > 0), signal=False)
                    MM(bank(6, h * 128, (h + 1) * 128), attnT[:, s_], vnew[:, s_], [T["attnT"], T["vnew"]], [BK[6]], start=False, stop=True,
                       acc=True, signal=(h == 3))
            for h in range(4):
                s_ = slice(h * 128, (h + 1) * 128)
                MM(bank(7, h * 128, (h + 1) * 128), kd[:, s_], vnew[:, s_], [T["kd"], T["vnew"]], [BK[7]], acc=(h > 0), signal=(h == 3))
            yield
            if not meta_tile:
                assert flags["z"], "z branch must be emitted before the DN gate"
                norm_gate(6, 0, 0, c_ss4, c_rs4, CT["ss4"], CT["rs4"])
            for h in range(4):
                s_ = slice(h * 128, (h + 1) * 128)
                I("dve", "scalar_tensor_tensor", [BK[7], TSA[2], T["Sd"]], [T["Sd"]], acc=True, out=Sd[:, s_], in0=Sd[:, s_],
                  scalar=egc[:, h * 128 + 127:h * 128 + 128], in1=bank(7, h * 128, (h + 1) * 128), op0=ALU.mult, op1=ALU.add)
            I("act", "activation", [T["Sd"]], [T["Sdb"]], out=Sdb[:], in_=Sd[:], func=AF.Copy)
            yield

        run_parallel([dn_chain(), gate_branch()] + list(carry))
        if meta_tile:
            return None
        for h in range(4):
            I("act", "activation", [BK[1]], [T["ATm"], CT["ss5"]], acc=True, out=ATm[:, h * 128:(h + 1) * 128],
              in_=bank(1, h * 128, (h + 1) * 128), func=AF.Square, accum_out=c_ss5[:, h:h + 1])
        rsqrt_cols(c_ss5, c_rs5, CT["ss5"], CT["rs5"], 1.0 / 128)
        for h in range(4):
            I("dve", "scalar_tensor_tensor", [BK[1], CT["rs5"], TSB[1]], [T["mix"]], acc=True,
              out=mix[:, 512 + h * 128:512 + (h + 1) * 128], in0=bank(1, h * 128, (h + 1) * 128),
              scalar=c_rs5[:, h:h + 1], in1=SB[:, 512 + h * 128:512 + (h + 1) * 128], op0=ALU.mult, op1=ALU.mult)

        def tail():
            for kc in range(8):
                TP(bankb(4, kc * 128, (kc + 1) * 128), mix[:, kc * 128:(kc + 1) * 128], identb[:], [T["mix"], T["identb"]], [BK[4]],
                   acc=(kc > 0), signal=(kc == 7))
            I("act", "activation", [BK[4]], [T["mixT"]], out=mixT[:], in_=bankb(4), func=AF.Copy)
            yield
            x1t = T["X1_%d" % mslot]
            x1 = X1[:, mslot * 1024:(mslot + 1) * 1024]
            for half, bk in enumerate((7, 4)):
                for kc in range(8):
                    MM(bank(bk), mixT[:, kc * 128:(kc + 1) * 128], wout[:, kc * D + half * 512:kc * D + (half + 1) * 512],
                       [T["mixT"], T["wout"]], [BK[bk]], start=(kc == 0), stop=(kc == 7), acc=(kc > 0), signal=(kc == 7))
                yield
                I("dve", "tensor_tensor", [txt, BK[bk]], [x1t], acc=True, out=x1[:, half * 512:(half + 1) * 512],
                  in0=xt[:, half * 512:(half + 1) * 512], in1=bank(bk), op=ALU.add)
            yield
            I("act", "activation", [x1t], [T["mixT"], CT["ss2"]], out=mixT[:], in_=x1, func=AF.Square, accum_out=c_ss2)
            rsqrt_cols(c_ss2, c_rs2, CT["ss2"], CT["rs2"], 1.0 / D)
            yield
            I("dve", "tensor_scalar", [x1t, CT["rs2"]], [T["hb2"]], out=hb2[:], in0=x1, scalar1=c_rs2, scalar2=None, op0=ALU.mult)
            for kc in range(8):
                TP(bankb(7, kc * 128, (kc + 1) * 128), hb2[:, kc * 128:(kc + 1) * 128], identb[:], [T["hb2"], T["identb"]], [BK[7]],
                   acc=(kc > 0), signal=(kc == 7))
            yield
            I("act", "activation", [BK[7]], [T["H2T"]], acc=True,
              out=H2T.rearrange("p (k t) -> p k t", k=8)[:, :, mslot * 128:(mslot + 1) * 128],
              in_=bankb(7).rearrange("p (k t) -> p k t", k=8), func=AF.Copy)
            yield

        return tail()

    def mlp_macro(seq, tile0, nt, carry):
        N = nt * 128
        for g_ in carry:
            for _ in g_:
                pass
        H3 = H2T.rearrange("p (k t) -> p k t", k=8)

        def load(g):
            sl = g % 2
            DMA("wu", WU[sl], wup_s[g], [T["wup_s"]], [T["WU%d" % sl]])
            DMA("wd", WD[sl], wdn_s[g], [T["wdn_s"]], [T["WD%d" % sl]], e="pool")

        def up(g):
            sl = g % 2
            for j in range(2):
                bk = 6 + j
                for kc in range(8):
                    MM(bank(bk, 0, N), WU[sl][:, kc * 256 + j * 128:kc * 256 + (j + 1) * 128], H3[:, kc, 0:N],
                       [T["WU%d" % sl], T["H2T"]], [BK[bk]], start=(kc == 0), stop=(kc == 7), acc=(kc > 0), signal=(kc == 7))
                I("act", "activation", [BK[bk]], [T["RL%d" % j]], out=RLs[j][:, 0:N], in_=bank(bk, 0, N), func=AF.Relu)
                I("dve", "tensor_tensor", [T["RL%d" % j]], [T["AT%d" % sl]], acc=(j > 0), out=ATb[sl][:, j * 384:j * 384 + N],
                  in0=RLs[j][:, 0:N], in1=RLs[j][:, 0:N], op=ALU.mult)

        def down(g):
            sl = g % 2
            for tt in range(nt):
                for half in range(2):
                    bk = tt * 2 + half
                    for j in range(2):
                        first = (g == 0 and j == 0)
                        last = (g == NGRP - 1 and j == 1)
                        MM(bank(bk), ATb[sl][:, j * 384 + tt * 128:j * 384 + (tt + 1) * 128],
                           WD[sl][:, j * 1024 + half * 512:j * 1024 + (half + 1) * 512], [T["AT%d" % sl], T["WD%d" % sl]], [BK[bk]],
                           start=first, stop=last, acc=(not first), signal=(last or (j == 1 and tt == nt - 1 and half == 1)))

        load(0)
        up(0)
        for g in range(NGRP):
            if g + 1 < NGRP:
                load(g + 1)
                up(g + 1)
            down(g)
        for tt in range(nt):
            x1t = T["X1_%d" % tt]
            x1 = X1[:, tt * 1024:(tt + 1) * 1024]
            for half in range(2):
                I("dve", "tensor_tensor", [x1t, BK[tt * 2 + half]], [x1t], acc=True, out=x1[:, half * 512:(half + 1) * 512],
                  in0=x1[:, half * 512:(half + 1) * 512], in1=bank(tt * 2 + half), op=ALU.add)

        def tail():
            for tt in range(nt):
                x1t = T["X1_%d" % tt]
                x1 = X1[:, tt * 1024:(tt + 1) * 1024]
                I("act", "activation", [x1t], [T["mixT"], CT["ss3"]], out=mixT[:], in_=x1, func=AF.Square, accum_out=c_ss3)
                rsqrt_cols(c_ss3, c_rs3, CT["ss3"], CT["rs3"], 1.0 / D)
                yield
                I("dve", "scalar_tensor_tensor", [x1t, CT["rs3"], T["gfbc"]], [x1t], out=x1, in0=x1, scalar=c_rs3, in1=gfbc[:],
                  op0=ALU.mult, op1=ALU.mult)
                tok0 = (tile0 + tt) * 128
                DMA("out", out_d[seq, tok0:tok0 + 128, :], x1, [x1t], [x1t])
                yield

        return tail()

    carry = []
    for seq in range(n_seq):
        mixer_tile(seq, 0, 0, carry)
        carry = []
        tile0 = 0
        for nt in macros:
            for tt in range(nt):
                tl = mixer_tile(seq, 1 + tile0 + tt, tt, carry)
                carry = [tl]
            tl = mlp_macro(seq, tile0, nt, carry)
            carry = [tl]
            tile0 += nt
    for g_ in carry:
        for _ in g_:
            pass
    I("pool", "memset", [T["X1_0"], T["X1_1"], T["X1_2"]], [CT["fin"]], ap=c_fin, constant=0.0)

    S.finalize()
    keys = list(S.ENG) + S.dmakeys
    sems = {k: es.enter_context(nc.semaphore("s_" + k)) for k in keys}
    block = es.enter_context(nc.Block())

    @block.tensor
    def _(eng):
        S.replay("pe", eng, sems)

    @block.scalar
    def _(eng):
        S.replay("act", eng, sems)

    @block.vector
    def _(eng):
        S.replay("dve", eng, sems)

    @block.gpsimd
    def _(eng):
        S.replay("pool", eng, sems)

    @block.sync
    def _(eng):
        S.replay("sp", eng, sems)

    es.close()
    S.sbtot = SBTOT[0]
    return nc, S


def make_consts():
    c = np.zeros((128, C_END), np.float32)
    s = np.arange(128)[:, None]
    cc = np.arange(128)[None, :]
    c[:, C_ID:C_ID + 128] = np.eye(128, dtype=np.float32)
    c[:, C_MUI:C_MUI + 512] = np.tile((cc >= s).astype(np.float32), (1, 4))
    c[:, C_MSU:C_MSU + 512] = np.tile((cc > s).astype(np.float32), (1, 4))
    c[:, C_NEG:C_NEG + 512] = np.tile(np.where(cc >= s, 0.0, -30000.0).astype(np.float32), (1, 4))
    for h in range(4):
        c[4 + h, C_SEL + h * 128:C_SEL + (h + 1) * 128] = 1.0
        c[4 + h, C_RM + h] = 1.0
    return c


def run(inputs, n_seq, macros, ncores):
    nc, S = build_program(n_seq, macros)
    n_real = sum(macros) * 128
    f = lambda a: np.ascontiguousarray(np.asarray(a, dtype=np.float32))
    x = f(inputs["x"])
    shared = {
        "meta": f(inputs["meta_tokens"]), "w_in": f(inputs["w_in"][0]), "w_out": f(inputs["w_out"][0]),
        "w_up": f(inputs["w_up"][0]), "w_down": f(inputs["w_down"][0]), "conv_w": f(inputs["conv_w"][0]),
        "consts": make_consts(), "norm1_g": f(inputs["norm1_g"][0]), "norm2_g": f(inputs["norm2_g"][0]),
        "final_norm_g": f(inputs["final_norm_g"]), "dn_norm_g": f(inputs["dn_norm_g"][0]),
        "gla_norm_g": f(inputs["gla_norm_g"][0]), "a_log": f(inputs["a_log"][0]), "dt_bias": f(inputs["dt_bias"][0]),
        "gla_w2": f(inputs["gla_w2"][0]), "gla_b": f(inputs["gla_b"][0]),
    }
    in_maps = []
    for c in range(ncores):
        m = dict(shared)
        m["x"] = np.ascontiguousarray(x[c * n_seq:(c + 1) * n_seq, :n_real])
        in_maps.append(m)
    res = run_bass_kernel_spmd(nc, in_maps, core_ids=list(range(ncores)))
    return np.concatenate([r["out"] for r in res.results], axis=0)


def kernel(**inputs):
    return run(inputs, 2, [3] * 10 + [2], NCORES)
```
